# Optimizing a Trainium2 kernel written in Bass

```python
import jax, jax.numpy as jnp
from jax import lax
import numpy as np

D_MODEL = 2048
BATCH = 2
SEQ = 8192
DEPTH = 4
DEC_BATCH = 2
DEC_SEQ = 4096
PAST_LEN = 128

N_MIXERS = 2
N_A_LAYERS = (DEPTH + 1) // 2
N_B_LAYERS = DEPTH // 2
CHUNK = 128
A_WIDTH = D_MODEL
A_GROUPS = 8
A_GROUP_DIM = A_WIDTH // A_GROUPS
HEAD_DIM = 128
N_HEADS = D_MODEL // HEAD_DIM
N_KV_HEADS = 4
Q_PER_KV = N_HEADS // N_KV_HEADS
WINDOW = 128
BLOCK = 128
N_BUCKETS = 32
MAX_DISTANCE = 128
D_FF = ((8 * D_MODEL // 3) + 255) // 256 * 256
CONV_WIDTH = 3
EPS = 1e-6

kernel_name = "hybrid_gmlp_swa_encoder"


def rmsnorm(x, g):
    xf = x.astype(jnp.float32)
    y = xf * lax.rsqrt(jnp.mean(xf * xf, axis=-1, keepdims=True) + EPS)
    return (y * g.astype(jnp.float32)).astype(x.dtype)


def _relative_bucket(rel):
    half = N_BUCKETS // 2
    max_exact = half // 2
    ret = (rel > 0).astype(np.int32) * half
    n = np.abs(rel)
    nf = np.maximum(n, 1).astype(np.float32)
    large = max_exact + (np.log(nf / max_exact) / np.log(MAX_DISTANCE / max_exact)
                         * (half - max_exact)).astype(np.int32)
    large = np.minimum(large, half - 1)
    return (ret + np.where(n < max_exact, n, large)).astype(np.int32)


def _band_bucket_mask(n_blocks, seq_len):
    a = np.arange(BLOCK)[:, None]
    c = np.arange(3 * BLOCK)[None, :]
    rel = c - BLOCK - a
    bucket = _relative_bucket(rel)
    in_window = np.abs(rel) <= WINDOW
    key_pos = np.arange(n_blocks)[:, None, None] * BLOCK + (c - BLOCK)[None]
    mask = (key_pos >= 0) & (key_pos < seq_len) & in_window[None]
    return bucket, mask


def mixer_a(h, w_in, b_in, v_norm, w_s, b_s, w_out):
    B, S, _ = h.shape
    nc = S // CHUNK
    z = jax.nn.gelu(h @ w_in + b_in)
    u, v = jnp.split(z, 2, axis=-1)
    v = rmsnorm(v, v_norm).reshape(B, nc, CHUNK, A_GROUPS, A_GROUP_DIM)
    s = jnp.einsum('gpq,bnqgc->bnpgc', w_s, v) + b_s.T[None, None, :, :, None]
    y = u * s.reshape(B, S, A_WIDTH)
    return y @ w_out


def mixer_b(h, rel_bias, w_qkv, sink, w_out):
    B, S, _ = h.shape
    nb = S // BLOCK
    qkv = h @ w_qkv
    q, k, v = jnp.split(qkv, [N_HEADS * HEAD_DIM, (N_HEADS + N_KV_HEADS) * HEAD_DIM], axis=-1)
    q = q.reshape(B, nb, BLOCK, N_KV_HEADS, Q_PER_KV, HEAD_DIM)

    def windows(t):
        t = t.reshape(B, S, N_KV_HEADS, HEAD_DIM)
        tp = jnp.pad(t, ((0, 0), (BLOCK, BLOCK), (0, 0), (0, 0)))
        tp = tp.reshape(B, nb + 2, BLOCK, N_KV_HEADS, HEAD_DIM)
        return jnp.concatenate([tp[:, :-2], tp[:, 1:-1], tp[:, 2:]], axis=2)

    kw, vw = windows(k), windows(v)
    bucket, mask = _band_bucket_mask(nb, S)
    bias = rel_bias.astype(jnp.float32)[bucket]
    bias = bias.transpose(2, 0, 1).reshape(N_KV_HEADS, Q_PER_KV, BLOCK, 3 * BLOCK)
    logits = jnp.einsum('bnqkgd,bnckd->bnkgqc', q, kw,
                        preferred_element_type=jnp.float32) * (HEAD_DIM ** -0.5) + bias
    logits = jnp.where(mask[None, :, None, None], logits, -jnp.inf)
    sink_l = sink.astype(jnp.float32).reshape(N_KV_HEADS, Q_PER_KV, 1, 1)
    m = jnp.maximum(jnp.max(logits, axis=-1, keepdims=True), sink_l)
    p = jnp.exp(logits - m)
    denom = jnp.sum(p, axis=-1, keepdims=True) + jnp.exp(sink_l - m)
    p = (p / denom).astype(vw.dtype)
    o = jnp.einsum('bnkgqc,bnckd->bnqkgd', p, vw).reshape(B, S, N_HEADS * HEAD_DIM)
    return o @ w_out


def conv_ffn(h, w_in, conv_w, conv_b, w_out):
    z = h @ w_in
    zp = jnp.pad(z, ((0, 0), (1, 1), (0, 0)))
    z = zp[:, :-2] * conv_w[0] + zp[:, 1:-1] * conv_w[1] + zp[:, 2:] * conv_w[2] + conv_b
    g, u = jnp.split(z, 2, axis=-1)
    return (jax.nn.silu(g) * u) @ w_out


def trunk(x, rel_bias, mix_norm, ffn_norm, final_norm,
          a_w_in, a_b_in, a_v_norm, a_w_s, a_b_s, a_w_out,
          b_w_qkv, b_sink, b_w_out,
          f_w_in, f_conv_w, f_conv_b, f_w_out):
    for i in range(DEPTH):
        h = rmsnorm(x, mix_norm[i])
        j = i // N_MIXERS
        if i % N_MIXERS == 0:
            x = x + mixer_a(h, a_w_in[j], a_b_in[j], a_v_norm[j], a_w_s[j], a_b_s[j], a_w_out[j])
        else:
            x = x + mixer_b(h, rel_bias, b_w_qkv[j], b_sink[j], b_w_out[j])
        h = rmsnorm(x, ffn_norm[i])
        x = x + conv_ffn(h, f_w_in[i], f_conv_w[i], f_conv_b[i], f_w_out[i])
    return rmsnorm(x, final_norm)


def setup_inputs(seed: int = 0) -> dict:
    key = jax.random.key(seed)
    ks = jax.random.split(key, 20)
    f32 = jnp.float32

    def nrm(k, shape, scale):
        return jax.random.normal(k, shape, f32) * scale

    qkv_dim = (N_HEADS + 2 * N_KV_HEADS) * HEAD_DIM
    return {
        "x_prompt": nrm(ks[0], (BATCH, SEQ, D_MODEL), 1.0),
        "x_sample": nrm(ks[1], (DEC_BATCH, DEC_SEQ, D_MODEL), 1.0),
        "rel_bias": nrm(ks[2], (N_BUCKETS, N_HEADS), 0.5),
        "mix_norm": 1.0 + nrm(ks[3], (DEPTH, D_MODEL), 0.02),
        "ffn_norm": 1.0 + nrm(ks[4], (DEPTH, D_MODEL), 0.02),
        "final_norm": 1.0 + nrm(ks[5], (D_MODEL,), 0.02),
        "a_w_in": nrm(ks[6], (N_A_LAYERS, D_MODEL, 2 * A_WIDTH), D_MODEL ** -0.5),
        "a_b_in": nrm(ks[7], (N_A_LAYERS, 2 * A_WIDTH), 0.02),
        "a_v_norm": 1.0 + nrm(ks[8], (N_A_LAYERS, A_WIDTH), 0.02),
        "a_w_s": nrm(ks[9], (N_A_LAYERS, A_GROUPS, CHUNK, CHUNK), CHUNK ** -0.5),
        "a_b_s": 1.0 + nrm(ks[10], (N_A_LAYERS, A_GROUPS, CHUNK), 0.02),
        "a_w_out": nrm(ks[11], (N_A_LAYERS, A_WIDTH, D_MODEL), A_WIDTH ** -0.5),
        "b_w_qkv": nrm(ks[12], (N_B_LAYERS, D_MODEL, qkv_dim), D_MODEL ** -0.5),
        "b_sink": nrm(ks[13], (N_B_LAYERS, N_HEADS), 0.5),
        "b_w_out": nrm(ks[14], (N_B_LAYERS, N_HEADS * HEAD_DIM, D_MODEL), (N_HEADS * HEAD_DIM) ** -0.5),
        "f_w_in": nrm(ks[15], (DEPTH, D_MODEL, 2 * D_FF), D_MODEL ** -0.5),
        "f_conv_w": nrm(ks[16], (DEPTH, CONV_WIDTH, 2 * D_FF), CONV_WIDTH ** -0.5),
        "f_conv_b": nrm(ks[17], (DEPTH, 2 * D_FF), 0.02),
        "f_w_out": nrm(ks[18], (DEPTH, D_FF, D_MODEL), D_FF ** -0.5),
    }


def reference(x_prompt, x_sample, rel_bias, mix_norm, ffn_norm, final_norm,
              a_w_in, a_b_in, a_v_norm, a_w_s, a_b_s, a_w_out,
              b_w_qkv, b_sink, b_w_out,
              f_w_in, f_conv_w, f_conv_b, f_w_out):
    y_prompt = trunk(x_prompt, rel_bias, mix_norm, ffn_norm, final_norm,
                     a_w_in, a_b_in, a_v_norm, a_w_s, a_b_s, a_w_out,
                     b_w_qkv, b_sink, b_w_out,
                     f_w_in, f_conv_w, f_conv_b, f_w_out)
    y_sample = trunk(x_sample, rel_bias, mix_norm, ffn_norm, final_norm,
                     a_w_in, a_b_in, a_v_norm, a_w_s, a_b_s, a_w_out,
                     b_w_qkv, b_sink, b_w_out,
                     f_w_in, f_conv_w, f_conv_b, f_w_out)
    return (y_prompt, y_sample)
```

```python
import numpy as np
import concourse.bass as bass
import concourse.mybir as mybir
from concourse.bass_utils import run_bass_kernel_spmd

F32 = mybir.dt.float32
BF16 = mybir.dt.bfloat16
AF = mybir.ActivationFunctionType
ALU = mybir.AluOpType

D = 2048
KC = 16
DFF = 5632
NFC = 44
T = 512
NEG = -30000.0
EPS = 1e-6
OWN = 3072
HALO = 512
SEQ_BOUNDS = (0, 8192, 16384, 20480, 24576)
NTOK = 24576


class Buf:
    __slots__ = ("name", "w", "r")

    def __init__(self, name=""):
        self.name = name
        self.w = None
        self.r = {}


class Sem:
    def __init__(self, h):
        self.h = h
        self.n = 0


class Op:
    __slots__ = ("eng", "fn", "deps", "inc", "sem", "val", "dma", "extra")


class Prog:
    ENGS = ("pe", "act", "dve", "pool", "sp")

    def __init__(self, esems, dsem_pool):
        self.q = {e: [] for e in self.ENGS}
        self.esem = esems
        self.dsems = [Sem(h) for h in dsem_pool]
        self.dfree = list(self.dsems)
        self.bufs = []
        self.nuid = 0

    def buf(self, name=""):
        b = Buf(name)
        self.bufs.append(b)
        return b

    def dsem(self):
        return self.dfree.pop()

    def add(self, eng, fn, r=(), w=(), dsem=None):
        o = Op()
        o.eng, o.fn, o.inc, o.extra = eng, fn, False, ()
        o.dma = dsem is not None
        deps = set()
        for b in r:
            if b.w is not None:
                deps.add(b.w)
        for b in w:
            if b.w is not None:
                deps.add(b.w)
            deps.update(b.r.values())
        deps.discard(o)
        for d in deps:
            d.inc = True
        o.deps = deps
        self.nuid += 1
        key = ("d", self.nuid) if o.dma else eng
        for b in r:
            b.r[key] = o
        for b in w:
            b.w = o
            b.r = {}
        if o.dma:
            dsem.n += 1
            o.sem, o.val = dsem, 16 * dsem.n
        else:
            o.sem, o.val = None, 0
        self.q[eng].append(o)
        return o

    def barrier(self):
        lasts = []
        for e in self.ENGS:
            for o in reversed(self.q[e]):
                if o.fn is not None:
                    if not o.dma:
                        o.inc = True
                        lasts.append(o)
                    break
        extra = tuple((s, 16 * s.n) for s in self.dsems if s.n > 0)
        for e in self.ENGS:
            o = Op()
            o.eng, o.fn, o.inc, o.dma = e, None, False, False
            o.deps, o.sem, o.val, o.extra = set(lasts), None, 0, extra
            self.q[e].append(o)
        for b in self.bufs:
            b.w = None
            b.r = {}
        self.bufs = []
        self.dfree = list(self.dsems)

    def finalize(self):
        for e in self.ENGS:
            s = self.esem[e]
            for o in self.q[e]:
                if o.inc and not o.dma and o.fn is not None:
                    s.n += 1
                    o.sem, o.val = s, s.n

    def emit(self, name, e):
        seen = {}
        for o in self.q[name]:
            waits = [(d.sem, d.val) for d in o.deps
                     if not (name == "pe" and d.eng == "pe" and not d.dma)]
            waits.extend(o.extra)
            for s, v in sorted(waits, key=lambda t: (id(t[0]), t[1])):
                if s is None:
                    continue
                k = id(s)
                if seen.get(k, 0) >= v:
                    continue
                e.wait_ge(s.h, v)
                seen[k] = v
            if o.fn is None:
                continue
            ins = o.fn(e)
            if o.dma:
                ins.then_inc(o.sem.h, 16)
            elif o.inc:
                ins.then_inc(o.sem.h, 1)


class Arena:
    def __init__(self, ap, nwords):
        self.ap, self.n, self.off = ap, nwords, 0

    def reset(self):
        self.off = 0

    def f32(self, n):
        n8 = (n + 7) // 8 * 8
        assert self.off + n8 <= self.n, ("arena overflow", self.off + n8, self.n)
        a = self.ap[:, self.off:self.off + n]
        self.off += n8
        return a

    def bf16(self, n):
        w = (n + 15) // 16 * 8
        return self.f32(w).bitcast(BF16)[:, 0:n]


def build(NT, layers, n_out_tok, out_off, trim=True):
    W = NT * T
    XW = W + 2
    NBLK = W // 128
    nc = bass.Bass("TRN2", target_bir_lowering=False)

    def din(name, shape, dt=F32):
        return nc.dram_tensor(name, list(shape), dt, kind="ExternalInput").ap()

    def dscr(name, shape, dt=BF16):
        return nc.dram_tensor(name, list(shape), dt, kind="Internal").ap()

    x_in = din("x_in", [D, XW])
    hmask = din("hmask", [128, NT])
    maskv = din("maskv", [128, NBLK * 3])
    gn = din("gn", [128, 9 * KC])
    cwd = din("cw", [128, 4 * 88 * 4])
    abu = din("abu", [128, 2 * KC])
    abv = din("abv", [2, 128, D])
    avn = din("avn", [2, 128, D])
    absr = din("absr", [2, 128, 8 * 128])
    wst = din("wst", [2, 128, 8 * 128])
    sink = din("sink", [2, 128, 16])
    rbr = din("rbr", [128, 32 * 16])
    bmask = din("bmask", [3, 128, 32 * 128])
    wmask = din("wmask", [128, 3 * 128])
    identd = din("ident", [128, 128])
    a_w_in = din("a_w_in", [2, D, 2 * D])
    a_w_out = din("a_w_out", [2, D, D])
    b_w_qkv = din("b_w_qkv", [2, D, 3072])
    b_w_out = din("b_w_out", [2, D, D])
    f_w_in = din("f_w_in", [4, D, 2 * DFF])
    f_w_out = din("f_w_out", [4, DFF, D])
    y_out = nc.dram_tensor("y_out", [D, n_out_tok], F32, kind="ExternalOutput").ap()

    xa = dscr("xa", [D, XW], F32)
    xb = dscr("xb", [D, XW], F32)
    W1G = dscr("W1G", [4, NFC, 128, KC * 128])
    W1U = dscr("W1U", [4, NFC, 128, KC * 128])
    W2 = dscr("W2", [4, KC, 128, NFC * 128])
    AU = dscr("AU", [2, KC, 128, KC * 128])
    AV = dscr("AV", [2, 8, 128, KC * 256])
    AO = dscr("AO", [2, KC, 128, KC * 128])
    BQ = dscr("BQ", [2, KC, 128, KC * 128])
    BK = dscr("BK", [2, 4, 128, KC * 128])
    BV = dscr("BV", [2, 1, 128, KC * 512])
    BO = dscr("BO", [2, KC, 128, KC * 128])
    biasd = dscr("biasd", [128, 2 * 3 * 16 * 128], BF16)

    ARW = 46400
    arena_t = nc.alloc_sbuf_tensor("arena", [128, ARW], F32)
    AR = Arena(arena_t.ap(), ARW)
    cst_t = nc.alloc_sbuf_tensor("cst", [128, 1984], F32)
    CST = Arena(cst_t.ap(), 1984)
    ps_t = nc.alloc_psum_tensor("ps", [128, 8 * 512], F32)
    PS = ps_t.ap()

    def bank(i):
        return PS[:, i * 512:(i + 1) * 512]

    import contextlib
    with contextlib.ExitStack() as es:
        sems = [es.enter_context(nc.semaphore("s%d" % i)) for i in range(45)]
        esems = {e: Sem(sems[i]) for i, e in enumerate(Prog.ENGS)}
        P = Prog(esems, sems[5:])
        pb = [P.buf("bank%d" % i) for i in range(8)]

        gn_s = CST.f32(9 * KC)
        cw_s = CST.f32(4 * 88 * 4)
        abu_s = CST.f32(2 * KC)
        hm_s = CST.f32(NT)
        mv_s = CST.f32(NBLK * 3)
        ones_f = CST.f32(128)
        eps_s = CST.f32(8)
        zero_s = CST.f32(8)
        ones_b = CST.bf16(128)
        cbuf = P.buf("consts")
        cds = P.dsem()
        for dst, src in ((gn_s, gn), (cw_s, cwd), (abu_s, abu), (hm_s, hmask), (mv_s, maskv)):
            P.add("sp", lambda e, dst=dst, src=src: e.dma_start(out=dst, in_=src), w=[cbuf], dsem=cds)
        P.add("dve", lambda e: e.memset(ones_f, 1.0), w=[cbuf])
        P.add("dve", lambda e: e.memset(ones_b, 1.0), w=[cbuf])
        P.add("dve", lambda e: e.memset(eps_s, EPS), w=[cbuf])
        P.add("dve", lambda e: e.memset(zero_s, 0.0), w=[cbuf])
        def steps_for(kind, idx, small):
            out = []

            def add(src2d, col0, Atot, CWd, NB, dst4, ch0):
                if not small:
                    out.append((src2d, col0, 0, Atot, CWd, NB, dst4, ch0))
                else:
                    for a0 in range(0, Atot, 4):
                        out.append((src2d, col0, a0, 4, CWd, NB, dst4, ch0))

            if kind == "F":
                l = idx
                for c4 in range(NFC // 4):
                    add(f_w_in[l], c4 * 512, KC, 128, 4, W1G[l], c4 * 4)
                    add(f_w_in[l], DFF + c4 * 512, KC, 128, 4, W1U[l], c4 * 4)
                if small:
                    for d4 in range(4):
                        add(f_w_out[l], d4 * 512, NFC, 128, 4, W2[l], d4 * 4)
                else:
                    for dc in range(KC):
                        add(f_w_out[l], dc * 128, NFC, 128, 1, W2[l], dc)
            elif kind == "A":
                j = idx
                for c4 in range(4):
                    add(a_w_in[j], c4 * 512, KC, 128, 4, AU[j], c4 * 4)
                    add(a_w_in[j], D + c4 * 512, KC, 256, 2, AV[j], c4 * 2)
                    add(a_w_out[j], c4 * 512, KC, 128, 4, AO[j], c4 * 4)
            else:
                j = idx
                for c4 in range(4):
                    add(b_w_qkv[j], c4 * 512, KC, 128, 4, BQ[j], c4 * 4)
                    add(b_w_out[j], c4 * 512, KC, 128, 4, BO[j], c4 * 4)
                add(b_w_qkv[j], 2048, KC, 128, 4, BK[j], 0)
                add(b_w_qkv[j], 2560, KC, 512, 1, BV[j], 0)
            return out

        def conv_load(step, sf_i, bf_i, dl_i):
            src2d, col0, a0, A, CWd, NB, dst4, ch0 = step
            n = A * NB * CWd
            src = src2d[a0 * 128:(a0 + A) * 128, col0:col0 + NB * CWd].rearrange("(a p) c -> p a c", p=128)
            f3 = sf_i[:, 0:n].rearrange("p (a c) -> p a c", a=A)
            P.add("sp", lambda e: e.dma_start(out=f3, in_=src), w=[bf_i], dsem=dl_i)

        def conv_cast_store(step, sf_i, sb_i, bf_i, bb_i, ds_i, ceng):
            src2d, col0, a0, A, CWd, NB, dst4, ch0 = step
            n = A * NB * CWd
            fin = sf_i[:, 0:n].rearrange("p (a n c) -> p n a c", a=A, n=NB)
            bout = sb_i[:, 0:n].rearrange("p (n a c) -> p n a c", n=NB, a=A)
            if ceng == "act":
                P.add("act", lambda e: e.copy(out=bout, in_=fin), r=[bf_i], w=[bb_i])
            else:
                P.add(ceng, lambda e: e.tensor_copy(out=bout, in_=fin), r=[bf_i], w=[bb_i])
            dd = dst4[ch0:ch0 + NB, :, a0 * CWd:(a0 + A) * CWd].rearrange("n p f -> p n f")
            bsrc = sb_i[:, 0:n].rearrange("p (n f) -> p n f", n=NB)
            P.add("pool", lambda e: e.dma_start(out=dd, in_=bsrc), r=[bb_i], dsem=ds_i)

        def prologue():
            AR.reset()
            zt = AR.f32(XW)
            zb_ = P.buf()
            zd = P.dsem()
            P.add("dve", lambda e: e.memset(zt, 0.0), w=[zb_])
            for xx in (xa, xb):
                for kc in range(KC):
                    P.add("pool", lambda e, xx=xx, kc=kc: e.dma_start(out=xx[kc * 128:(kc + 1) * 128, :], in_=zt),
                          r=[zb_], dsem=zd)
            bias_parts = []
            if any(k == "B" for k, _ in layers):
                rb_s = AR.f32(512)
                wm_s = AR.f32(384)
                bm_s = AR.f32(4096)
                acc = AR.f32(3 * 16 * 128)
                hi_ = AR.bf16(6144)
                lo_ = AR.bf16(6144)
                b1, b2 = P.buf(), P.buf()
                accb = [P.buf() for _ in range(48)]
                d1, d2, d3 = P.dsem(), P.dsem(), P.dsem()
                P.add("sp", lambda e: e.dma_start(out=rb_s, in_=rbr), w=[b1], dsem=d1)
                P.add("sp", lambda e: e.dma_start(out=wm_s, in_=wmask), w=[b1], dsem=d1)

                def bias_part(jj):
                    P.add("sp", lambda e: e.dma_start(out=bm_s, in_=bmask[jj]), w=[b2], dsem=d2)
                    for h in range(16):
                        o_ = acc[:, (jj * 16 + h) * 128:(jj * 16 + h + 1) * 128]
                        P.add("dve", lambda e, o_=o_: e.tensor_copy(
                            out=o_, in_=wm_s[:, jj * 128:(jj + 1) * 128]), r=[b1], w=[accb[jj * 16 + h]])
                    for b in range(32):
                        for h in range(16):
                            o_ = acc[:, (jj * 16 + h) * 128:(jj * 16 + h + 1) * 128]
                            P.add("dve", lambda e, o_=o_, b=b, h=h: e.scalar_tensor_tensor(
                                out=o_, in0=bm_s[:, b * 128:(b + 1) * 128],
                                scalar=rb_s[:, b * 16 + h:b * 16 + h + 1], in1=o_,
                                op0=ALU.mult, op1=ALU.add), r=[b1, b2], w=[accb[jj * 16 + h]])

                def bias_finish():
                    hlb = P.buf()
                    P.add("dve", lambda e: e.tensor_scalar(out=acc, in0=acc, scalar1=float(128.0 ** 0.5), scalar2=None,
                                                           op0=ALU.mult), r=accb, w=accb)
                    P.add("dve", lambda e: e.tensor_copy(out=hi_, in_=acc), r=accb, w=[hlb])
                    P.add("dve", lambda e: e.tensor_tensor(out=lo_, in0=acc, in1=hi_, op=ALU.subtract),
                          r=accb + [hlb], w=[hlb])
                    P.add("sp", lambda e: e.dma_start(out=biasd[:, 0:6144], in_=hi_), r=[hlb], dsem=d3)
                    P.add("sp", lambda e: e.dma_start(out=biasd[:, 6144:12288], in_=lo_), r=[hlb], dsem=d3)

                bias_parts = [lambda: bias_part(0), lambda: bias_part(1), lambda: bias_part(2), bias_finish]

            NST = 2
            sf = [AR.f32(8192) for _ in range(NST)]
            sb = [AR.bf16(8192) for _ in range(NST)]
            bf_ = [P.buf() for _ in range(NST)]
            bb_ = [P.buf() for _ in range(NST)]
            dl = [P.dsem() for _ in range(NST)]
            dst_ = [P.dsem() for _ in range(NST)]
            steps = steps_for(layers[0][0], layers[0][1], False) + steps_for("F", 0, False)
            every = max(1, len(steps) // 4)
            for k, stp in enumerate(steps):
                if k % every == 0 and bias_parts:
                    bias_parts.pop(0)()
                i = k % NST
                conv_load(stp, sf[i], bf_[i], dl[i])
                conv_cast_store(stp, sf[i], sb[i], bf_[i], bb_[i], dst_[i], "act")
            while bias_parts:
                bias_parts.pop(0)()
            P.barrier()

        prologue()

        def norm_a_steps(xsrc, t, st, jl=0, jh=T):
            c0 = t * T + 1
            xr, xrb, xrd, sq, sqb = st["xr"], st["xrb"], st["xrd"], st["sq"], st["sqb"]
            acc, accb = st["acc"], st["accb"]
            steps = []
            for kc in range(KC):
                box = {}

                def load(kc=kc, box=box):
                    i = st["n"] % len(xr)
                    st["n"] += 1
                    box["i"] = i
                    P.add("pool", lambda e: e.dma_start(
                        out=xr[i][:, jl:jh], in_=xsrc[kc * 128:(kc + 1) * 128, c0 + jl:c0 + jh]), w=[xrb[i]], dsem=xrd[i])

                def sqf(kc=kc, box=box):
                    i = box["i"]
                    k2 = kc % 2
                    if kc < 2:
                        P.add("act", lambda e: e.activation(out=acc[k2][:, jl:jh], in_=xr[i][:, jl:jh], func=AF.Square),
                              r=[xrb[i]], w=[accb[k2]])
                    else:
                        P.add("act", lambda e: e.activation(out=sq[k2][:, jl:jh], in_=xr[i][:, jl:jh], func=AF.Square),
                              r=[xrb[i]], w=[sqb[k2]])
                        P.add("dve", lambda e: e.tensor_tensor(out=acc[k2][:, jl:jh], in0=acc[k2][:, jl:jh],
                                                               in1=sq[k2][:, jl:jh], op=ALU.add),
                              r=[sqb[k2], accb[k2]], w=[accb[k2]])
                steps.append((load, sqf))
            return steps

        def norm_a(xsrc, t, st, jl=0, jh=T):
            for ld_, sq_ in norm_a_steps(xsrc, t, st, jl, jh):
                ld_()
                sq_()

        def norm_b(xsrc, t, gi, hT, hT_b, st, msb=7, jl=0, jh=T):
            c0 = t * T + 1
            xr, xrb, xrd, rstd, rstdb = st["xr"], st["xrb"], st["xrd"], st["rstd"], st["rstdb"]
            acc, accb = st["acc"], st["accb"]
            ms = bank(msb)
            for k2 in range(2):
                P.add("pe", lambda e, k2=k2: e.matmul(ms[:, jl:jh], lhsT=ones_f, rhs=acc[k2][:, jl:jh],
                                                     start=(k2 == 0), stop=(k2 == 1)),
                      r=[accb[k2], cbuf], w=[pb[msb]])
            P.add("act", lambda e: e.activation(out=rstd[:, jl:jh], in_=ms[:, jl:jh], func=AF.Sqrt, scale=1.0 / D,
                                                bias=eps_s[:, 0:1]), r=[pb[msb], cbuf], w=[rstdb])
            P.add("dve", lambda e: e.reciprocal(out=rstd[:, jl:jh], in_=rstd[:, jl:jh]), r=[rstdb], w=[rstdb])
            for kc in range(KC):
                i = st["n"] % len(xr)
                st["n"] += 1
                P.add("pool", lambda e, i=i, kc=kc: e.dma_start(
                    out=xr[i][:, jl:jh], in_=xsrc[kc * 128:(kc + 1) * 128, c0 + jl:c0 + jh]), w=[xrb[i]], dsem=xrd[i])
                P.add("dve", lambda e, i=i, kc=kc: e.scalar_tensor_tensor(
                    out=hT[:, kc * T + jl:kc * T + jh], in0=xr[i][:, jl:jh],
                    scalar=gn_s[:, gi * KC + kc:gi * KC + kc + 1], in1=rstd[:, jl:jh],
                    op0=ALU.mult, op1=ALU.mult), r=[xrb[i], rstdb, cbuf], w=[hT_b[kc]])

        def norm_state(NX=4):
            return dict(xr=[AR.f32(T) for _ in range(NX)], xrb=[P.buf() for _ in range(NX)],
                        xrd=[P.dsem() for _ in range(NX)],
                        sq=[AR.f32(T) for _ in range(2)], sqb=[P.buf() for _ in range(2)],
                        acc=[AR.f32(T) for _ in range(2)], accb=[P.buf() for _ in range(2)],
                        rstd=AR.f32(T), rstdb=P.buf(), n=0)

        def resid_state(NR=3):
            return dict(xr=[AR.f32(T) for _ in range(NR)], xb=[P.buf() for _ in range(NR)],
                        xd=[P.dsem() for _ in range(NR)], n=0)

        def resid_store(rs, xsrc, xdst, dc, t, obank, ml=0, mh=T):
            i = rs["n"] % len(rs["xr"])
            rs["n"] += 1
            xr_, xb_, xd_ = rs["xr"][i], rs["xb"][i], rs["xd"][i]
            c0 = rs["c0"](t)
            P.add("pool", lambda e: e.dma_start(out=xr_[:, ml:mh], in_=xsrc[dc * 128:(dc + 1) * 128, c0 + ml:c0 + mh]),
                  w=[xb_], dsem=xd_)
            P.add("dve", lambda e: e.tensor_tensor(out=xr_[:, ml:mh], in0=bank(obank)[:, ml:mh], in1=xr_[:, ml:mh], op=ALU.add),
                  r=[pb[obank], xb_], w=[xb_])
            P.add("pool", lambda e: e.dma_start(out=xdst[dc * 128:(dc + 1) * 128, c0 + ml:c0 + mh], in_=xr_[:, ml:mh]),
                  r=[xb_], dsem=xd_)

        def wring(n, words):
            return dict(t=[AR.bf16(words) for _ in range(n)], b=[P.buf() for _ in range(n)],
                        d=[P.dsem() for _ in range(n)], n=0)

        def wload(wr, src):
            i = wr["n"] % len(wr["t"])
            wr["n"] += 1
            tl, b, d = wr["t"][i], wr["b"][i], wr["d"][i]
            n = src.shape[-1]
            P.add("sp", lambda e: e.dma_start(out=tl[:, 0:n], in_=src), w=[b], dsem=d)
            return tl, b

        def ffn(l, xsrc, xdst, bg_steps=(), ranges=None):
            AR.reset()
            NBG = 3
            bgf = [AR.f32(2048) for _ in range(NBG)]
            bgb = [AR.bf16(2048) for _ in range(NBG)]
            bgfb = [P.buf() for _ in range(NBG)]
            bgbb = [P.buf() for _ in range(NBG)]
            bgdl = [P.dsem() for _ in range(NBG)]
            bgds = [P.dsem() for _ in range(NBG)]
            bgq = list(bg_steps)
            bgk = [0]
            pend = []

            def bg_tick():
                if pend:
                    stp, i = pend.pop()
                    conv_cast_store(stp, bgf[i], bgb[i], bgfb[i], bgbb[i], bgds[i], "act")
                if bgk[0] < len(bgq):
                    i = bgk[0] % NBG
                    stp = bgq[bgk[0]]
                    bgk[0] += 1
                    conv_load(stp, bgf[i], bgfb[i], bgdl[i])
                    pend.append((stp, i))

            hT2 = [AR.bf16(KC * T) for _ in range(2)]
            hT2_b = [[P.buf() for _ in range(KC)] for _ in range(2)]
            a = AR.bf16(NFC * T)
            a_b = [P.buf() for _ in range(NFC)]
            w1 = wring(3, KC * 128)
            w2 = wring(2, NFC * 128)
            st = norm_state(3)
            rs = resid_state()
            rs["c0"] = lambda t: t * T
            tg = [AR.f32(T) for _ in range(2)]
            tu = [AR.f32(T) for _ in range(2)]
            tgb = [P.buf() for _ in range(2)]
            tub = [P.buf() for _ in range(2)]
            zs = AR.f32(88 * 2)
            zsb = [P.buf() for _ in range(88)]
            wm = AR.f32(88 * 2)
            wmb = P.buf()
            P.add("dve", lambda e: e.memset(zs, 0.0), w=zsb)
            cwl = cw_s[:, l * 352:(l + 1) * 352]

            def cwc(ch, k):
                return cwl[:, ch * 4 + k:ch * 4 + k + 1]

            rg = ranges if ranges is not None else [("full", 0, T)] * NT

            def nrange(t):
                kind, jl, jh = rg[t]
                if kind == "zonly":
                    return max(0, jl - 126) // 128 * 128, jh
                return jl, jh

            norm_a(xsrc, 0, st, *nrange(0))
            norm_b(xsrc, 0, 4 + l, hT2[0], hT2_b[0], st, 7, *nrange(0))
            for t in range(NT):
                hT, hT_b = hT2[t % 2], hT2_b[t % 2]
                kind, jl, jh = rg[t]
                zonly = (kind == "zonly")
                ml, mh = (0 if jl == 0 else jl + 2), jh
                cw3 = cwl.rearrange("p (c k) -> p c k", k=4)
                wm3 = wm.rearrange("p (c k) -> p c k", k=2)
                P.add("dve", lambda e, t=t: e.tensor_scalar(out=wm3[:, :, 0:1], in0=cw3[:, :, 0:1],
                                                          scalar1=hm_s[:, t:t + 1], scalar2=None, op0=ALU.mult),
                      r=[cbuf], w=[wmb])
                P.add("dve", lambda e, t=t: e.tensor_scalar(out=wm3[:, :, 1:2], in0=cw3[:, :, 2:3],
                                                          scalar1=hm_s[:, t:t + 1], scalar2=None, op0=ALU.mult),
                      r=[cbuf], w=[wmb])
                for fc in range(NFC):
                    par = fc % 2
                    zb = {}
                    if fc % 2 == 0:
                        bg_tick()
                    for gu, Wd in ((0, W1G), (1, W1U)):
                        wt, wb_ = wload(w1, Wd[l, fc])
                        bk = par * 2 + gu
                        zb[gu] = bk
                        for kc in range(KC):
                            P.add("pe", lambda e, wt=wt, kc=kc, bk=bk, hT=hT, jl=jl, jh=jh: e.matmul(
                                bank(bk)[:, jl:jh], lhsT=wt[:, kc * 128:(kc + 1) * 128],
                                rhs=hT[:, kc * T + jl:kc * T + jh],
                                start=(kc == 0), stop=(kc == KC - 1)), r=[wb_, hT_b[kc]], w=[pb[bk]])
                    for gu in (0, 1):
                        ch = gu * NFC + fc
                        Pz = bank(zb[gu])
                        pbk = pb[zb[gu]]
                        tt = (tg if gu == 0 else tu)[par]
                        ttb = (tgb if gu == 0 else tub)[par]
                        s0 = zs[:, ch * 2:ch * 2 + 1]
                        s1 = zs[:, ch * 2 + 1:ch * 2 + 2]
                        w0, w1_, w2_, bb = cwc(ch, 0), cwc(ch, 1), cwc(ch, 2), cwc(ch, 3)
                        w0m = wm[:, ch * 2:ch * 2 + 1]
                        w2m = wm[:, ch * 2 + 1:ch * 2 + 2]
                        if not zonly:
                            P.add("act", lambda e, tt=tt, Pz=Pz, w1_=w1_, bb=bb, jl=jl, jh=jh: e.activation(
                                out=tt[:, jl + 1:jh], in_=Pz[:, jl:jh - 1], func=AF.Identity, scale=w1_, bias=bb),
                                r=[pbk, cbuf], w=[ttb])
                            if jl == 0:
                                P.add("act", lambda e, tt=tt, s1=s1, w1_=w1_, bb=bb: e.activation(
                                    out=tt[:, 0:1], in_=s1, func=AF.Identity, scale=w1_, bias=bb),
                                    r=[zsb[ch], cbuf], w=[ttb])
                            P.add("dve", lambda e, tt=tt, Pz=Pz, w0=w0, jl=jl, jh=jh: e.scalar_tensor_tensor(
                                out=tt[:, jl + 2:jh], in0=Pz[:, jl:jh - 2], scalar=w0, in1=tt[:, jl + 2:jh],
                                op0=ALU.mult, op1=ALU.add), r=[pbk, cbuf], w=[ttb])
                            P.add("dve", lambda e, tt=tt, Pz=Pz, w2_=w2_, jl=jl, jh=jh: e.scalar_tensor_tensor(
                                out=tt[:, jl + 1:jh], in0=Pz[:, jl + 1:jh], scalar=w2_, in1=tt[:, jl + 1:jh],
                                op0=ALU.mult, op1=ALU.add), r=[pbk, cbuf], w=[ttb])
                            if jl == 0:
                                P.add("dve", lambda e, tt=tt, s0=s0, w0=w0: e.scalar_tensor_tensor(
                                    out=tt[:, 0:1], in0=s0, scalar=w0, in1=tt[:, 0:1],
                                    op0=ALU.mult, op1=ALU.add), r=[zsb[ch], cbuf], w=[ttb])
                                P.add("dve", lambda e, tt=tt, Pz=Pz, w2m=w2m: e.scalar_tensor_tensor(
                                    out=tt[:, 0:1], in0=Pz[:, 0:1], scalar=w2m, in1=tt[:, 0:1],
                                    op0=ALU.mult, op1=ALU.add), r=[pbk, wmb], w=[ttb])
                                P.add("dve", lambda e, tt=tt, s1=s1, w0m=w0m: e.scalar_tensor_tensor(
                                    out=tt[:, 1:2], in0=s1, scalar=w0m, in1=tt[:, 1:2],
                                    op0=ALU.mult, op1=ALU.add), r=[zsb[ch], wmb], w=[ttb])
                        if jh == T:
                            P.add("dve", lambda e, Pz=Pz, ch=ch: e.tensor_copy(
                                out=zs[:, ch * 2:ch * 2 + 2], in_=Pz[:, T - 2:T]), r=[pbk], w=[zsb[ch]])
                    if zonly:
                        continue
                    P.add("act", lambda e, par=par, ml=ml, mh=mh: e.activation(
                        out=tg[par][:, ml:mh], in_=tg[par][:, ml:mh], func=AF.Silu),
                        r=[tgb[par]], w=[tgb[par]])
                    P.add("dve", lambda e, par=par, fc=fc, ml=ml, mh=mh: e.tensor_tensor(
                        out=a[:, fc * T + ml:fc * T + mh], in0=tg[par][:, ml:mh], in1=tu[par][:, ml:mh], op=ALU.mult),
                        r=[tgb[par], tub[par]], w=[a_b[fc]])
                if t + 1 < NT:
                    norm_a(xsrc, t + 1, st, *nrange(t + 1))
                for dc in range(KC):
                    if dc == 4 and t + 1 < NT:
                        norm_b(xsrc, t + 1, 4 + l, hT2[(t + 1) % 2], hT2_b[(t + 1) % 2], st, 7, *nrange(t + 1))
                    if dc % 2 == 0:
                        bg_tick()
                    if zonly:
                        continue
                    wt, wb_ = wload(w2, W2[l, dc])
                    ob = 4 + dc % 2
                    for fc in range(NFC):
                        P.add("pe", lambda e, wt=wt, fc=fc, ob=ob, ml=ml, mh=mh: e.matmul(
                            bank(ob)[:, ml:mh], lhsT=wt[:, fc * 128:(fc + 1) * 128], rhs=a[:, fc * T + ml:fc * T + mh],
                            start=(fc == 0), stop=(fc == NFC - 1)), r=[wb_, a_b[fc]], w=[pb[ob]])
                    resid_store(rs, xsrc, xdst, dc, t, ob, ml, mh)
            while pend or bgk[0] < len(bgq):
                bg_tick()
            P.barrier()

        def mixer_a(j, gi, xsrc, xdst, blocks=None):
            AR.reset()
            hT2 = [AR.bf16(KC * T) for _ in range(2)]
            hT2_b = [[P.buf() for _ in range(KC)] for _ in range(2)]
            vn = AR.bf16(4 * D)
            vn_b = [P.buf() for _ in range(4)]
            yT = AR.bf16(KC * T)
            yT_b = [P.buf() for _ in range(KC)]
            wu = wring(3, KC * 128)
            wv = wring(3, KC * 256)
            st = norm_state()
            rs = resid_state()
            rs["c0"] = lambda t: t * T + 1
            bvb = AR.f32(D)
            vnb = AR.f32(D)
            bsb = AR.f32(1024)
            wsf = AR.f32(1024)
            wsb = AR.bf16(1024)
            lb = P.buf()
            ld = P.dsem()
            P.add("sp", lambda e: e.dma_start(out=bvb, in_=abv[j]), w=[lb], dsem=ld)
            P.add("sp", lambda e: e.dma_start(out=vnb, in_=avn[j]), w=[lb], dsem=ld)
            P.add("sp", lambda e: e.dma_start(out=bsb, in_=absr[j]), w=[lb], dsem=ld)
            P.add("sp", lambda e: e.dma_start(out=wsf, in_=wst[j]), w=[lb], dsem=ld)
            P.add("dve", lambda e: e.tensor_copy(out=wsb, in_=wsf), r=[lb], w=[lb])
            vg4 = [yT.bitcast(F32)[:, 0:D], yT.bitcast(F32)[:, D:2 * D], AR.f32(D), AR.f32(D)]
            vg4b = [[P.buf()] + yT_b[0:8], [P.buf()] + yT_b[8:16], [P.buf()], [P.buf()]]
            ssq = AR.f32(16)
            ssb = P.buf()
            uc_ = [AR.f32(T) for _ in range(2)]
            ucb = [P.buf() for _ in range(2)]
            tmp = [AR.f32(T) for _ in range(2)]
            tmb = [P.buf() for _ in range(2)]
            junk = AR.bf16(D)
            jb = P.buf()
            norm_a(xsrc, 0, st)
            norm_b(xsrc, 0, gi, hT2[0], hT2_b[0], st)
            for t in range(NT):
                hT, hT_b = hT2[t % 2], hT2_b[t % 2]
                bl = (blocks or {}).get(t, [0, 1, 2, 3])
                cl, ch_ = bl[0] * 128, (bl[-1] + 1) * 128
                for cg in range(8):
                    wt, wb_ = wload(wv, AV[j, cg])
                    for b in bl:
                        bk = (cg * 4 + b) % 2
                        for kc in range(KC):
                            P.add("pe", lambda e, wt=wt, kc=kc, bk=bk, b=b, hT=hT: e.matmul(
                                bank(bk)[:, 0:256], lhsT=hT[:, kc * T + b * 128:kc * T + (b + 1) * 128],
                                rhs=wt[:, kc * 256:(kc + 1) * 256],
                                start=(kc == 0), stop=(kc == KC - 1)), r=[wb_, hT_b[kc]], w=[pb[bk]])
                        P.add("dve", lambda e, bk=bk, cg=cg, b=b: e.tensor_tensor(
                            out=vg4[b][:, cg * 256:(cg + 1) * 256], in0=bank(bk)[:, 0:256],
                            in1=bvb[:, cg * 256:(cg + 1) * 256], op=ALU.add),
                            r=[pb[bk], lb], w=vg4b[b])
                nsteps = norm_a_steps(xsrc, t + 1, st) if t + 1 < NT else []
                for b in bl:
                    P.add("act", lambda e, b=b: e.activation(out=vg4[b], in_=vg4[b], func=AF.Gelu_apprx_tanh),
                          r=vg4b[b], w=vg4b[b])
                    P.add("act", lambda e, b=b: e.activation(out=junk, in_=vg4[b], func=AF.Square,
                                                             accum_out=ssq[:, b:b + 1]),
                          r=vg4b[b], w=[jb, ssb])
                P.add("act", lambda e: e.activation(out=ssq[:, 4:8], in_=ssq[:, 0:4], func=AF.Sqrt,
                                                    scale=1.0 / D, bias=eps_s[:, 0:1]), r=[ssb, cbuf], w=[ssb])
                P.add("dve", lambda e: e.reciprocal(out=ssq[:, 8:12], in_=ssq[:, 4:8]), r=[ssb], w=[ssb])
                for b in bl:
                    P.add("dve", lambda e, b=b: e.scalar_tensor_tensor(
                        out=vn[:, b * D:(b + 1) * D], in0=vg4[b], scalar=ssq[:, 8 + b:9 + b], in1=vnb,
                        op0=ALU.mult, op1=ALU.mult), r=vg4b[b] + [ssb, lb], w=[vn_b[b]])
                for cc in range(KC):
                    wt, wb_ = wload(wu, AU[j, cc])
                    ub = 2 + cc % 2
                    sbk = 4 + cc % 2
                    p2 = cc % 2
                    if nsteps:
                        nsteps[cc][0]()
                        if cc >= 1:
                            nsteps[cc - 1][1]()
                    for kc in range(KC):
                        P.add("pe", lambda e, wt=wt, kc=kc, ub=ub, hT=hT, cl=cl, ch_=ch_: e.matmul(
                            bank(ub)[:, cl:ch_], lhsT=wt[:, kc * 128:(kc + 1) * 128], rhs=hT[:, kc * T + cl:kc * T + ch_],
                            start=(kc == 0), stop=(kc == KC - 1)), r=[wb_, hT_b[kc]], w=[pb[ub]])
                    g = cc // 2
                    for b in bl:
                        P.add("pe", lambda e, b=b, cc=cc, g=g, sbk=sbk: e.matmul(
                            bank(sbk)[:, b * 128:(b + 1) * 128],
                            lhsT=vn[:, b * D + cc * 128:b * D + (cc + 1) * 128],
                            rhs=wsb[:, g * 128:(g + 1) * 128], start=True, stop=True),
                            r=[vn_b[b], lb], w=[pb[sbk]])
                    P.add("act", lambda e, ub=ub, p2=p2, cc=cc, cl=cl, ch_=ch_: e.activation(
                        out=uc_[p2][:, cl:ch_], in_=bank(ub)[:, cl:ch_], func=AF.Gelu_apprx_tanh,
                        bias=abu_s[:, j * KC + cc:j * KC + cc + 1]), r=[pb[ub], cbuf], w=[ucb[p2]])
                    for b in bl:
                        P.add("dve", lambda e, b=b, g=g, sbk=sbk, p2=p2: e.tensor_tensor(
                            out=tmp[p2][:, b * 128:(b + 1) * 128], in0=bank(sbk)[:, b * 128:(b + 1) * 128],
                            in1=bsb[:, g * 128:(g + 1) * 128], op=ALU.add), r=[pb[sbk], lb], w=[tmb[p2]])
                    P.add("dve", lambda e, p2=p2, cc=cc, cl=cl, ch_=ch_: e.tensor_tensor(
                        out=yT[:, cc * T + cl:cc * T + ch_], in0=tmp[p2][:, cl:ch_], in1=uc_[p2][:, cl:ch_], op=ALU.mult),
                        r=[tmb[p2], ucb[p2]], w=[yT_b[cc]])
                if nsteps:
                    nsteps[KC - 1][1]()
                for dc in range(KC):
                    if dc == 4 and t + 1 < NT:
                        norm_b(xsrc, t + 1, gi, hT2[(t + 1) % 2], hT2_b[(t + 1) % 2], st)
                    wt, wb_ = wload(wu, AO[j, dc])
                    ob = 6 if dc % 2 == 0 else 0
                    for kc in range(KC):
                        P.add("pe", lambda e, wt=wt, kc=kc, ob=ob, cl=cl, ch_=ch_: e.matmul(
                            bank(ob)[:, cl:ch_], lhsT=wt[:, kc * 128:(kc + 1) * 128], rhs=yT[:, kc * T + cl:kc * T + ch_],
                            start=(kc == 0), stop=(kc == KC - 1)), r=[wb_, yT_b[kc]], w=[pb[ob]])
                    resid_store(rs, xsrc, xdst, dc, t, ob, cl, ch_)
            P.barrier()

        def mixer_b(j, gi, xsrc, xdst, qblocks=None):
            AR.reset()
            RT = 3
            RB = RT * 4
            hT2 = [AR.bf16(KC * T) for _ in range(2)]
            hT2_b = [[P.buf() for _ in range(KC)] for _ in range(2)]
            KT = AR.bf16(4 * RB * 128)
            KT_b = [[P.buf() for _ in range(4)] for _ in range(RT)]
            Vt = AR.bf16(RB * 512)
            V_b = [P.buf() for _ in range(RB)]
            qT = AR.bf16(KC * T)
            qT_b = [P.buf() for _ in range(KC)]
            oall = AR.bf16(KC * T)
            oall_b = [P.buf() for _ in range(KC)]
            wq = wring(3, KC * 128)
            wvr = wring(1, KC * 512)
            st = norm_state(3)
            rs = resid_state(2)
            rs["c0"] = lambda t: t * T + 1
            bT = AR.bf16(2 * 3 * 16 * 128)
            idf = AR.f32(128)
            idb = AR.bf16(128)
            es_ = AR.f32(16)
            lb = P.buf()
            ld = P.dsem()
            P.add("sp", lambda e: e.dma_start(out=bT, in_=biasd), w=[lb], dsem=ld)
            P.add("sp", lambda e: e.dma_start(out=es_, in_=sink[j]), w=[lb], dsem=ld)
            P.add("act", lambda e: e.activation(out=es_, in_=es_, func=AF.Exp), r=[lb], w=[lb])
            P.add("sp", lambda e: e.dma_start(out=idf, in_=identd), w=[lb], dsem=ld)
            P.add("dve", lambda e: e.tensor_copy(out=idb, in_=idf), r=[lb], w=[lb])
            NE = 8
            pT = [AR.bf16(T) for _ in range(NE)]
            pTb = [P.buf() for _ in range(NE)]
            ds_ = [AR.f32(T) for _ in range(2)]
            dsb = [P.buf() for _ in range(2)]
            vt, vtb = wload(wvr, BV[j, 0])
            scale = 128.0 ** -0.5
            cnt = {"u": 0, "g": 0}

            def rp(kb):
                return ((kb // 4) % RT) * 4 + kb % 4

            def phase1(t):
                hp = t % 2
                slot = t % RT
                for kh in range(4):
                    wt, wb_ = wload(wq, BK[j, kh])
                    bk = kh % 2
                    for kc in range(KC):
                        P.add("pe", lambda e, wt=wt, kc=kc, bk=bk, hp=hp: e.matmul(
                            bank(bk), lhsT=wt[:, kc * 128:(kc + 1) * 128], rhs=hT2[hp][:, kc * T:(kc + 1) * T],
                            start=(kc == 0), stop=(kc == KC - 1)), r=[wb_, hT2_b[hp][kc]], w=[pb[bk]])
                    o_ = KT[:, kh * RB * 128 + slot * 512:kh * RB * 128 + (slot + 1) * 512]
                    P.add("act", lambda e, o_=o_, bk=bk: e.copy(out=o_, in_=bank(bk)), r=[pb[bk]], w=[KT_b[slot][kh]])
                for b in range(4):
                    bk = 2 + b % 2
                    gb = rp(t * 4 + b)
                    for kc in range(KC):
                        P.add("pe", lambda e, kc=kc, bk=bk, b=b, hp=hp: e.matmul(
                            bank(bk), lhsT=hT2[hp][:, kc * T + b * 128:kc * T + (b + 1) * 128],
                            rhs=vt[:, kc * 512:(kc + 1) * 512], start=(kc == 0), stop=(kc == KC - 1)),
                            r=[vtb, hT2_b[hp][kc]], w=[pb[bk]])
                    P.add("dve", lambda e, gb=gb, bk=bk: e.tensor_copy(out=Vt[:, gb * 512:(gb + 1) * 512], in_=bank(bk)),
                          r=[pb[bk]], w=[V_b[gb]])

            def logits(u):
                b, kh, i, jj, n_, nj, g = u["b"], u["kh"], u["i"], u["jj"], u["n"], u["nj"], u["g"]
                s_ = cnt["u"]
                cnt["u"] += 1
                u["s"] = s_
                lbk, ei = s_ % 4, s_ % NE
                kb = i - 1 + jj
                kslot = (kb // 4) % RT
                kcol = kh * RB * 128 + rp(kb) * 128
                q3 = qT.rearrange("p (h n) -> p h n", h=KC)[:, kh * 4:(kh + 1) * 4, b * 128:(b + 1) * 128]
                qr = [qT_b[kh * 4 + hh] for hh in range(4)]
                o3_ = bank(lbk).rearrange("p (h n) -> p h n", h=4)
                P.add("pe", lambda e: e.matmul(o3_, lhsT=KT[:, kcol:kcol + 128], rhs=q3,
                                               start=True, stop=False), r=[KT_b[kslot][kh]] + qr, w=[pb[lbk]])
                for hl in range(2):
                    b3 = bT.rearrange("p (s j h n) -> p s j h n", s=2, j=3, h=16)[:, hl, jj, kh * 4:(kh + 1) * 4, :]
                    P.add("pe", lambda e, b3=b3, hl=hl: e.matmul(o3_, lhsT=idb, rhs=b3, start=False, stop=(hl == 1)),
                          r=[lb], w=[pb[lbk]])
                P.add("act", lambda e: e.activation(
                    out=pT[ei], in_=bank(lbk), func=AF.Exp, scale=scale, bias=mv_s[:, i * 3 + jj:i * 3 + jj + 1]),
                    r=[pb[lbk], cbuf], w=[pTb[ei]])

            def pvden(u, t):
                b, kh, i, jj, n_, nj, g = u["b"], u["kh"], u["i"], u["jj"], u["n"], u["nj"], u["g"]
                ei = u["s"] % NE
                ob = 4 + 2 * (g % 2)
                db = ob + 1
                di = g % 2
                vb_ = rp(i - 1 + jj)
                P.add("pe", lambda e: e.matmul(
                    bank(ob), lhsT=Vt[:, vb_ * 512 + kh * 128:vb_ * 512 + (kh + 1) * 128], rhs=pT[ei],
                    start=(n_ == 0), stop=(n_ == nj - 1)), r=[V_b[vb_], pTb[ei]], w=[pb[ob]])
                P.add("pe", lambda e: e.matmul(
                    bank(db), lhsT=ones_b, rhs=pT[ei],
                    start=(n_ == 0), stop=(n_ == nj - 1)), r=[cbuf, pTb[ei]], w=[pb[db]])
                if n_ != nj - 1:
                    return
                for hh in range(4):
                    h = kh * 4 + hh
                    P.add("act", lambda e, hh=hh, h=h: e.activation(
                        out=ds_[di][:, hh * 128:(hh + 1) * 128], in_=bank(db)[:, hh * 128:(hh + 1) * 128],
                        func=AF.Ln, bias=es_[:, h:h + 1]), r=[pb[db], lb], w=[dsb[di]])
                P.add("act", lambda e: e.activation(out=ds_[di], in_=ds_[di], func=AF.Exp, scale=-1.0),
                      r=[dsb[di]], w=[dsb[di]])
                o3 = oall.rearrange("p (h n) -> p h n", h=KC)[:, kh * 4:(kh + 1) * 4, b * 128:(b + 1) * 128]
                P.add("dve", lambda e: e.tensor_tensor(
                    out=o3, in0=bank(ob).rearrange("p (h n) -> p h n", h=4),
                    in1=ds_[di].rearrange("p (h n) -> p h n", h=4), op=ALU.mult),
                    r=[pb[ob], dsb[di]], w=[oall_b[kh * 4 + hh] for hh in range(4)])

            def phase2(t):
                hp = t % 2
                bl = (qblocks or {}).get(t, [0, 1, 2, 3])
                cl, ch_ = bl[0] * 128, (bl[-1] + 1) * 128
                for hq in range(KC):
                    wt, wb_ = wload(wq, BQ[j, hq])
                    bk = hq % 2
                    for kc in range(KC):
                        P.add("pe", lambda e, wt=wt, kc=kc, bk=bk, hp=hp: e.matmul(
                            bank(bk)[:, cl:ch_], lhsT=wt[:, kc * 128:(kc + 1) * 128],
                            rhs=hT2[hp][:, kc * T + cl:kc * T + ch_],
                            start=(kc == 0), stop=(kc == KC - 1)), r=[wb_, hT2_b[hp][kc]], w=[pb[bk]])
                    P.add("act", lambda e, hq=hq, bk=bk: e.copy(out=qT[:, hq * T + cl:hq * T + ch_], in_=bank(bk)[:, cl:ch_]),
                          r=[pb[bk]], w=[qT_b[hq]])
                nsteps = norm_a_steps(xsrc, t + 2, st) if t + 2 < NT else []
                units = []
                for b in bl:
                    i = t * 4 + b
                    for kh in range(4):
                        js = [jj for jj in range(3) if 0 <= i - 1 + jj < NBLK]
                        g = cnt["g"]
                        cnt["g"] += 1
                        for n_, jj in enumerate(js):
                            units.append(dict(b=b, kh=kh, i=i, jj=jj, n=n_, nj=len(js), g=g))
                SK = 4
                nb_at = min(26, max(len(units) + SK, KC + 2) - 1)
                for k in range(max(len(units) + SK, KC + 2 if nsteps else 0)):
                    if nsteps and k < KC:
                        nsteps[k][0]()
                    if nsteps and 2 <= k < KC + 2:
                        nsteps[k - 2][1]()
                    if k == nb_at and t + 2 < NT:
                        norm_b(xsrc, t + 2, gi, hT2[hp], hT2_b[hp], st, msb=3)
                    if k < len(units):
                        logits(units[k])
                    if 0 <= k - SK < len(units):
                        pvden(units[k - SK], t)
                for dc in range(KC):
                    wt, wb_ = wload(wq, BO[j, dc])
                    ob = 2 + dc % 2
                    for kc in range(KC):
                        P.add("pe", lambda e, wt=wt, kc=kc, ob=ob: e.matmul(
                            bank(ob)[:, cl:ch_], lhsT=wt[:, kc * 128:(kc + 1) * 128], rhs=oall[:, kc * T + cl:kc * T + ch_],
                            start=(kc == 0), stop=(kc == KC - 1)), r=[wb_, oall_b[kc]], w=[pb[ob]])
                    resid_store(rs, xsrc, xdst, dc, t, ob, cl, ch_)

            norm_a(xsrc, 0, st)
            norm_b(xsrc, 0, gi, hT2[0], hT2_b[0], st)
            phase1(0)
            if NT > 1:
                norm_a(xsrc, 1, st)
                norm_b(xsrc, 1, gi, hT2[1], hT2_b[1], st)
            for t in range(NT):
                if t + 1 < NT:
                    phase1(t + 1)
                phase2(t)
            P.barrier()

        def final_norm(xsrc):
            AR.reset()
            xt = [AR.f32(KC * T) for _ in range(2)]
            xtb = [P.buf() for _ in range(2)]
            xtd = [P.dsem() for _ in range(2)]
            sq = [AR.f32(T) for _ in range(2)]
            sqb = [P.buf() for _ in range(2)]
            acc = AR.f32(T)
            accb = P.buf()
            rstd = AR.f32(T)
            rstdb = P.buf()
            n = 0
            for t in range(NT):
                lo = max(t * T, out_off)
                hi = min((t + 1) * T, out_off + n_out_tok)
                if lo >= hi:
                    continue
                c0 = t * T + 1
                i = n % 2
                n += 1
                x3 = xt[i].rearrange("p (k n) -> p k n", k=KC)
                src = xsrc[:, c0:c0 + T].rearrange("(k p) n -> p k n", p=128)
                P.add("sp", lambda e, x3=x3, src=src: e.dma_start(out=x3, in_=src), w=[xtb[i]], dsem=xtd[i])
                for kc in range(KC):
                    k2 = kc % 2
                    xk = xt[i][:, kc * T:(kc + 1) * T]
                    if kc == 0:
                        P.add("act", lambda e, xk=xk: e.activation(out=acc, in_=xk, func=AF.Square), r=[xtb[i]], w=[accb])
                    else:
                        P.add("act", lambda e, xk=xk, k2=k2: e.activation(out=sq[k2], in_=xk, func=AF.Square),
                              r=[xtb[i]], w=[sqb[k2]])
                        P.add("dve", lambda e, k2=k2: e.tensor_tensor(out=acc, in0=acc, in1=sq[k2], op=ALU.add),
                              r=[sqb[k2], accb], w=[accb])
                P.add("pe", lambda e: e.matmul(bank(7), lhsT=ones_f, rhs=acc, start=True, stop=True),
                      r=[accb, cbuf], w=[pb[7]])
                P.add("act", lambda e: e.activation(out=rstd, in_=bank(7), func=AF.Sqrt, scale=1.0 / D,
                                                    bias=eps_s[:, 0:1]), r=[pb[7], cbuf], w=[rstdb])
                P.add("dve", lambda e: e.reciprocal(out=rstd, in_=rstd), r=[rstdb], w=[rstdb])
                for kc in range(KC):
                    xk = xt[i][:, kc * T:(kc + 1) * T]
                    P.add("dve", lambda e, xk=xk, kc=kc: e.scalar_tensor_tensor(
                        out=xk, in0=xk, scalar=gn_s[:, 8 * KC + kc:8 * KC + kc + 1], in1=rstd,
                        op0=ALU.mult, op1=ALU.mult), r=[xtb[i], rstdb, cbuf], w=[xtb[i]])
                dst = y_out[:, lo - out_off:hi - out_off].rearrange("(k p) n -> p k n", p=128)
                P.add("sp", lambda e, x3=x3, dst=dst, lo=lo, hi=hi, t=t: e.dma_start(
                    out=dst, in_=x3[:, :, lo - t * T:hi - t * T]), r=[xtb[i]], dsem=xtd[i])
            P.barrier()

        cur, nxt = x_in, xa
        for l, (kind, j) in enumerate(layers):
            msub = None
            if isinstance(trim, dict):
                msub = trim.get(("M", l))
            elif trim and NT == 8 and len(layers) == 4:
                msub = {1: {0: [1, 2, 3], 7: [0, 1, 2]}, 2: {0: [2, 3], 7: [0, 1]}, 3: {0: [3], 7: [0]}}.get(l)
            if kind == "A":
                mixer_a(j, l, cur, nxt, msub)
            else:
                mixer_b(j, l, cur, nxt, msub)
            cur, nxt = nxt, (xb if nxt is xa else xa)
            bg = []
            if l + 1 < len(layers):
                bg = steps_for(layers[l + 1][0], layers[l + 1][1], True) + steps_for("F", l + 1, True)
            rgs = None
            if isinstance(trim, dict):
                rgs = trim.get(l)
            elif trim and NT == 8 and len(layers) == 4:
                lo_hi = {0: (124, 388), 1: (252, 260), 2: (380, 132)}
                if l < 3:
                    jl0, jh7 = lo_hi[l]
                    rgs = [("full", jl0, T)] + [("full", 0, T)] * 6 + [("full", 0, jh7)]
                else:
                    rgs = [("zonly", 510, T)] + [("full", 0, T)] * 6 + [("full", 0, 4)]
            ffn(l, cur, nxt, bg, rgs)
            cur, nxt = nxt, (xb if nxt is xa else xa)
        final_norm(cur)

        P.finalize()
        with nc.Block() as block:
            @block.tensor
            def _(e):
                P.emit("pe", e)

            @block.scalar
            def _(e):
                P.emit("act", e)

            @block.vector
            def _(e):
                P.emit("dve", e)

            @block.gpsimd
            def _(e):
                P.emit("pool", e)

            @block.sync
            def _(e):
                P.emit("sp", e)
    return nc


def _relative_bucket(rel):
    half, max_exact = 16, 8
    ret = (rel > 0).astype(np.int32) * half
    n = np.abs(rel)
    nf = np.maximum(n, 1).astype(np.float32)
    large = max_exact + (np.log(nf / max_exact) / np.log(128 / max_exact) * (half - max_exact)).astype(np.int32)
    large = np.minimum(large, half - 1)
    return (ret + np.where(n < max_exact, n, large)).astype(np.int32)


def static_tables():
    c = np.arange(128)[:, None]
    q = np.arange(128)[None, :]
    bm = np.zeros((3, 128, 32, 128), np.float32)
    wm = np.zeros((128, 3, 128), np.float32)
    for jj in range(3):
        rel = (jj - 1) * 128 + c - q
        bk = _relative_bucket(rel)
        for b in range(32):
            bm[jj, :, b, :] = (bk == b)
        wm[:, jj, :] = np.where(np.abs(rel) <= 128, 0.0, NEG)
    return bm.reshape(3, 128, 32 * 128), wm.reshape(128, 3 * 128)


def per_partition(v):
    v = np.asarray(v, np.float32)
    lead = v.shape[:-1]
    n = v.shape[-1] // 128
    return np.ascontiguousarray(np.moveaxis(v.reshape(*lead, n, 128), -1, 0))


def shared_inputs(inp):
    f = np.float32
    d = {}
    gn = np.concatenate([inp["mix_norm"], inp["ffn_norm"], inp["final_norm"][None]], 0)
    d["gn"] = per_partition(gn).reshape(128, 9 * KC)
    cw = np.concatenate([inp["f_conv_w"], inp["f_conv_b"][:, None, :]], 1)
    cwp = per_partition(cw)
    d["cw"] = np.ascontiguousarray(cwp.transpose(0, 1, 3, 2)).reshape(128, 4 * 88 * 4)
    d["abu"] = per_partition(inp["a_b_in"][:, :D]).reshape(128, 2 * KC)
    d["abv"] = np.ascontiguousarray(np.broadcast_to(inp["a_b_in"][:, None, D:], (2, 128, D))).astype(f)
    d["avn"] = np.ascontiguousarray(np.broadcast_to(inp["a_v_norm"][:, None, :], (2, 128, D))).astype(f)
    d["absr"] = np.ascontiguousarray(np.broadcast_to(inp["a_b_s"].reshape(2, 1, 1024), (2, 128, 1024))).astype(f)
    d["wst"] = np.ascontiguousarray(np.asarray(inp["a_w_s"], f).transpose(0, 3, 1, 2)).reshape(2, 128, 1024)
    d["sink"] = np.ascontiguousarray(np.broadcast_to(np.asarray(inp["b_sink"], f)[:, None, :], (2, 128, 16)))
    d["rbr"] = np.ascontiguousarray(np.broadcast_to(np.asarray(inp["rel_bias"], f).reshape(1, 512), (128, 512)))
    bm, wm = static_tables()
    d["bmask"], d["wmask"] = bm, wm
    d["ident"] = np.eye(128, dtype=np.float32)
    for k in ("a_w_in", "a_w_out", "b_w_qkv", "b_w_out", "f_w_in", "f_w_out"):
        d[k] = np.ascontiguousarray(np.asarray(inp[k], f))
    return d


def window_inputs(xflat_T, start, NT, bounds, ntok):
    W = NT * T
    xw = np.zeros((D, W + 2), np.float32)
    lo, hi = max(start, 0), min(start + W, ntok)
    if hi > lo:
        xw[:, 1 + lo - start:1 + hi - start] = xflat_T[:, lo:hi]
    bset = set(bounds)
    hm = np.ones((NT,), np.float32)
    for t in range(NT):
        if t == 0 or (start + t * T) in bset:
            hm[t] = 0.0
    NBLK = W // 128
    mv = np.zeros((NBLK, 3), np.float32)
    for i in range(NBLK):
        s = start + i * 128
        if s in bset or i == 0:
            mv[i, 0] = NEG
        if (s + 128) in bset or i == NBLK - 1:
            mv[i, 2] = NEG
    return {"x_in": xw,
            "hmask": np.ascontiguousarray(np.broadcast_to(hm[None], (128, NT))),
            "maskv": np.ascontiguousarray(np.broadcast_to(mv.reshape(1, -1), (128, NBLK * 3)))}


_CACHE = {}


def kernel(**inp):
    NT = (OWN + 2 * HALO) // T
    layers = (("A", 0), ("B", 0), ("A", 1), ("B", 1))
    key = "full"
    if key not in _CACHE:
        _CACHE[key] = build(NT, layers, OWN, HALO)
    nc = _CACHE[key]
    xp = np.asarray(inp["x_prompt"], np.float32).reshape(-1, D)
    xs = np.asarray(inp["x_sample"], np.float32).reshape(-1, D)
    xflat_T = np.ascontiguousarray(np.concatenate([xp, xs], 0).T)
    sh = shared_inputs(inp)
    in_maps = []
    for c in range(8):
        m = dict(sh)
        m.update(window_inputs(xflat_T, c * OWN - HALO, NT, SEQ_BOUNDS, NTOK))
        in_maps.append(m)
    res = run_bass_kernel_spmd(nc, in_maps, core_ids=list(range(8)))
    yT = np.concatenate([np.asarray(r["y_out"]) for r in res.results], axis=1)
    y = np.ascontiguousarray(yT.T)
    n_p = xp.shape[0]
    return (y[:n_p].reshape(inp["x_prompt"].shape).astype(np.float32),
            y[n_p:].reshape(inp["x_sample"].shape).astype(np.float32))
```

```python
import numpy as np
import concourse.bass as bass
import concourse.mybir as mybir
from concourse.bass_utils import run_bass_kernel_spmd

F32 = mybir.dt.float32
BF16 = mybir.dt.bfloat16
AF = mybir.ActivationFunctionType
ALU = mybir.AluOpType

D = 2048
KC = 16
DFF = 5632
NFC = 44
T = 512
NEG = -30000.0
EPS = 1e-6
OWN = 3072
HALO = 512
SEQ_BOUNDS = (0, 8192, 16384, 20480, 24576)
NTOK = 24576


class Buf:
    __slots__ = ("name", "w", "r")

    def __init__(self, name=""):
        self.name = name
        self.w = None
        self.r = {}


class Sem:
    def __init__(self, h):
        self.h = h
        self.n = 0


class Op:
    __slots__ = ("eng", "fn", "deps", "inc", "sem", "val", "dma", "extra")


class Prog:
    ENGS = ("pe", "act", "dve", "pool", "sp")

    def __init__(self, esems, dsem_pool):
        self.q = {e: [] for e in self.ENGS}
        self.esem = esems
        self.dsems = [Sem(h) for h in dsem_pool]
        self.dfree = list(self.dsems)
        self.bufs = []
        self.nuid = 0

    def buf(self, name=""):
        b = Buf(name)
        self.bufs.append(b)
        return b

    def dsem(self):
        return self.dfree.pop()

    def add(self, eng, fn, r=(), w=(), dsem=None):
        o = Op()
        o.eng, o.fn, o.inc, o.extra = eng, fn, False, ()
        o.dma = dsem is not None
        deps = set()
        for b in r:
            if b.w is not None:
                deps.add(b.w)
        for b in w:
            if b.w is not None:
                deps.add(b.w)
            deps.update(b.r.values())
        deps.discard(o)
        for d in deps:
            d.inc = True
        o.deps = deps
        self.nuid += 1
        key = ("d", self.nuid) if o.dma else eng
        for b in r:
            b.r[key] = o
        for b in w:
            b.w = o
            b.r = {}
        if o.dma:
            dsem.n += 1
            o.sem, o.val = dsem, 16 * dsem.n
        else:
            o.sem, o.val = None, 0
        self.q[eng].append(o)
        return o

    def barrier(self):
        lasts = []
        for e in self.ENGS:
            for o in reversed(self.q[e]):
                if o.fn is not None:
                    if not o.dma:
                        o.inc = True
                        lasts.append(o)
                    break
        extra = tuple((s, 16 * s.n) for s in self.dsems if s.n > 0)
        for e in self.ENGS:
            o = Op()
            o.eng, o.fn, o.inc, o.dma = e, None, False, False
            o.deps, o.sem, o.val, o.extra = set(lasts), None, 0, extra
            self.q[e].append(o)
        for b in self.bufs:
            b.w = None
            b.r = {}
        self.bufs = []
        self.dfree = list(self.dsems)

    def finalize(self):
        for e in self.ENGS:
            s = self.esem[e]
            for o in self.q[e]:
                if o.inc and not o.dma and o.fn is not None:
                    s.n += 1
                    o.sem, o.val = s, s.n

    def emit(self, name, e):
        seen = {}
        for o in self.q[name]:
            waits = [(d.sem, d.val) for d in o.deps
                     if not (name == "pe" and d.eng == "pe" and not d.dma)]
            waits.extend(o.extra)
            for s, v in sorted(waits, key=lambda t: (id(t[0]), t[1])):
                if s is None:
                    continue
                k = id(s)
                if seen.get(k, 0) >= v:
                    continue
                e.wait_ge(s.h, v)
                seen[k] = v
            if o.fn is None:
                continue
            ins = o.fn(e)
            if o.dma:
                ins.then_inc(o.sem.h, 16)
            elif o.inc:
                ins.then_inc(o.sem.h, 1)


class Arena:
    def __init__(self, ap, nwords):
        self.ap, self.n, self.off = ap, nwords, 0

    def reset(self):
        self.off = 0

    def f32(self, n):
        n8 = (n + 7) // 8 * 8
        assert self.off + n8 <= self.n, ("arena overflow", self.off + n8, self.n)
        a = self.ap[:, self.off:self.off + n]
        self.off += n8
        return a

    def bf16(self, n):
        w = (n + 15) // 16 * 8
        return self.f32(w).bitcast(BF16)[:, 0:n]


def build(NT, layers, n_out_tok, out_off, trim=True):
    W = NT * T
    XW = W + 2
    NBLK = W // 128
    nc = bass.Bass("TRN2", target_bir_lowering=False)

    def din(name, shape, dt=F32):
        return nc.dram_tensor(name, list(shape), dt, kind="ExternalInput").ap()

    def dscr(name, shape, dt=BF16):
        return nc.dram_tensor(name, list(shape), dt, kind="Internal").ap()

    x_in = din("x_in", [D, XW])
    hmask = din("hmask", [128, NT])
    maskv = din("maskv", [128, NBLK * 3])
    gn = din("gn", [128, 9 * KC])
    cwd = din("cw", [128, 4 * 88 * 4])
    abu = din("abu", [128, 2 * KC])
    abv = din("abv", [2, 128, D])
    avn = din("avn", [2, 128, D])
    absr = din("absr", [2, 128, 8 * 128])
    wst = din("wst", [2, 128, 8 * 128])
    sink = din("sink", [2, 128, 16])
    rbr = din("rbr", [128, 32 * 16])
    bmask = din("bmask", [3, 128, 32 * 128])
    wmask = din("wmask", [128, 3 * 128])
    identd = din("ident", [128, 128])
    a_w_in = din("a_w_in", [2, D, 2 * D])
    a_w_out = din("a_w_out", [2, D, D])
    b_w_qkv = din("b_w_qkv", [2, D, 3072])
    b_w_out = din("b_w_out", [2, D, D])
    f_w_in = din("f_w_in", [4, D, 2 * DFF])
    f_w_out = din("f_w_out", [4, DFF, D])
    y_out = nc.dram_tensor("y_out", [D, n_out_tok], F32, kind="ExternalOutput").ap()

    xa = dscr("xa", [D, XW], F32)
    xb = dscr("xb", [D, XW], F32)
    W1G = dscr("W1G", [4, NFC, 128, KC * 128])
    W1U = dscr("W1U", [4, NFC, 128, KC * 128])
    W2 = dscr("W2", [4, KC, 128, NFC * 128])
    AU = dscr("AU", [2, KC, 128, KC * 128])
    AV = dscr("AV", [2, 8, 128, KC * 256])
    AO = dscr("AO", [2, KC, 128, KC * 128])
    BQ = dscr("BQ", [2, KC, 128, KC * 128])
    BK = dscr("BK", [2, 4, 128, KC * 128])
    BV = dscr("BV", [2, 1, 128, KC * 512])
    BO = dscr("BO", [2, KC, 128, KC * 128])
    biasd = dscr("biasd", [128, 2 * 3 * 16 * 128], BF16)

    ARW = 46400
    arena_t = nc.alloc_sbuf_tensor("arena", [128, ARW], F32)
    AR = Arena(arena_t.ap(), ARW)
    cst_t = nc.alloc_sbuf_tensor("cst", [128, 1984], F32)
    CST = Arena(cst_t.ap(), 1984)
    ps_t = nc.alloc_psum_tensor("ps", [128, 8 * 512], F32)
    PS = ps_t.ap()

    def bank(i):
        return PS[:, i * 512:(i + 1) * 512]

    import contextlib
    with contextlib.ExitStack() as es:
        sems = [es.enter_context(nc.semaphore("s%d" % i)) for i in range(45)]
        esems = {e: Sem(sems[i]) for i, e in enumerate(Prog.ENGS)}
        P = Prog(esems, sems[5:])
        pb = [P.buf("bank%d" % i) for i in range(8)]

        gn_s = CST.f32(9 * KC)
        cw_s = CST.f32(4 * 88 * 4)
        abu_s = CST.f32(2 * KC)
        hm_s = CST.f32(NT)
        mv_s = CST.f32(NBLK * 3)
        ones_f = CST.f32(128)
        eps_s = CST.f32(8)
        zero_s = CST.f32(8)
        ones_b = CST.bf16(128)
        cbuf = P.buf("consts")
        cds = P.dsem()
        for dst, src in ((gn_s, gn), (cw_s, cwd), (abu_s, abu), (hm_s, hmask), (mv_s, maskv)):
            P.add("sp", lambda e, dst=dst, src=src: e.dma_start(out=dst, in_=src), w=[cbuf], dsem=cds)
        P.add("dve", lambda e: e.memset(ones_f, 1.0), w=[cbuf])
        P.add("dve", lambda e: e.memset(ones_b, 1.0), w=[cbuf])
        P.add("dve", lambda e: e.memset(eps_s, EPS), w=[cbuf])
        P.add("dve", lambda e: e.memset(zero_s, 0.0), w=[cbuf])
        def steps_for(kind, idx, small):
            out = []

            def add(src2d, col0, Atot, CWd, NB, dst4, ch0):
                if not small:
                    out.append((src2d, col0, 0, Atot, CWd, NB, dst4, ch0))
                else:
                    for a0 in range(0, Atot, 4):
                        out.append((src2d, col0, a0, 4, CWd, NB, dst4, ch0))

            if kind == "F":
                l = idx
                for c4 in range(NFC // 4):
                    add(f_w_in[l], c4 * 512, KC, 128, 4, W1G[l], c4 * 4)
                    add(f_w_in[l], DFF + c4 * 512, KC, 128, 4, W1U[l], c4 * 4)
                if small:
                    for d4 in range(4):
                        add(f_w_out[l], d4 * 512, NFC, 128, 4, W2[l], d4 * 4)
                else:
                    for dc in range(KC):
                        add(f_w_out[l], dc * 128, NFC, 128, 1, W2[l], dc)
            elif kind == "A":
                j = idx
                for c4 in range(4):
                    add(a_w_in[j], c4 * 512, KC, 128, 4, AU[j], c4 * 4)
                    add(a_w_in[j], D + c4 * 512, KC, 256, 2, AV[j], c4 * 2)
                    add(a_w_out[j], c4 * 512, KC, 128, 4, AO[j], c4 * 4)
            else:
                j = idx
                for c4 in range(4):
                    add(b_w_qkv[j], c4 * 512, KC, 128, 4, BQ[j], c4 * 4)
                    add(b_w_out[j], c4 * 512, KC, 128, 4, BO[j], c4 * 4)
                add(b_w_qkv[j], 2048, KC, 128, 4, BK[j], 0)
                add(b_w_qkv[j], 2560, KC, 512, 1, BV[j], 0)
            return out

        def conv_load(step, sf_i, bf_i, dl_i):
            src2d, col0, a0, A, CWd, NB, dst4, ch0 = step
            n = A * NB * CWd
            src = src2d[a0 * 128:(a0 + A) * 128, col0:col0 + NB * CWd].rearrange("(a p) c -> p a c", p=128)
            f3 = sf_i[:, 0:n].rearrange("p (a c) -> p a c", a=A)
            P.add("sp", lambda e: e.dma_start(out=f3, in_=src), w=[bf_i], dsem=dl_i)

        def conv_cast_store(step, sf_i, sb_i, bf_i, bb_i, ds_i, ceng):
            src2d, col0, a0, A, CWd, NB, dst4, ch0 = step
            n = A * NB * CWd
            fin = sf_i[:, 0:n].rearrange("p (a n c) -> p n a c", a=A, n=NB)
            bout = sb_i[:, 0:n].rearrange("p (n a c) -> p n a c", n=NB, a=A)
            if ceng == "act":
                P.add("act", lambda e: e.copy(out=bout, in_=fin), r=[bf_i], w=[bb_i])
            else:
                P.add(ceng, lambda e: e.tensor_copy(out=bout, in_=fin), r=[bf_i], w=[bb_i])
            dd = dst4[ch0:ch0 + NB, :, a0 * CWd:(a0 + A) * CWd].rearrange("n p f -> p n f")
            bsrc = sb_i[:, 0:n].rearrange("p (n f) -> p n f", n=NB)
            P.add("pool", lambda e: e.dma_start(out=dd, in_=bsrc), r=[bb_i], dsem=ds_i)

        def prologue():
            AR.reset()
            zt = AR.f32(XW)
            zb_ = P.buf()
            zd = P.dsem()
            P.add("dve", lambda e: e.memset(zt, 0.0), w=[zb_])
            for xx in (xa, xb):
                for kc in range(KC):
                    P.add("pool", lambda e, xx=xx, kc=kc: e.dma_start(out=xx[kc * 128:(kc + 1) * 128, :], in_=zt),
                          r=[zb_], dsem=zd)
            bias_parts = []
            if any(k == "B" for k, _ in layers):
                rb_s = AR.f32(512)
                wm_s = AR.f32(384)
                bm_s = AR.f32(4096)
                acc = AR.f32(3 * 16 * 128)
                hi_ = AR.bf16(6144)
                lo_ = AR.bf16(6144)
                b1, b2 = P.buf(), P.buf()
                accb = [P.buf() for _ in range(48)]
                d1, d2, d3 = P.dsem(), P.dsem(), P.dsem()
                P.add("sp", lambda e: e.dma_start(out=rb_s, in_=rbr), w=[b1], dsem=d1)
                P.add("sp", lambda e: e.dma_start(out=wm_s, in_=wmask), w=[b1], dsem=d1)

                def bias_part(jj):
                    P.add("sp", lambda e: e.dma_start(out=bm_s, in_=bmask[jj]), w=[b2], dsem=d2)
                    for h in range(16):
                        o_ = acc[:, (jj * 16 + h) * 128:(jj * 16 + h + 1) * 128]
                        P.add("dve", lambda e, o_=o_: e.tensor_copy(
                            out=o_, in_=wm_s[:, jj * 128:(jj + 1) * 128]), r=[b1], w=[accb[jj * 16 + h]])
                    for b in range(32):
                        for h in range(16):
                            o_ = acc[:, (jj * 16 + h) * 128:(jj * 16 + h + 1) * 128]
                            P.add("dve", lambda e, o_=o_, b=b, h=h: e.scalar_tensor_tensor(
                                out=o_, in0=bm_s[:, b * 128:(b + 1) * 128],
                                scalar=rb_s[:, b * 16 + h:b * 16 + h + 1], in1=o_,
                                op0=ALU.mult, op1=ALU.add), r=[b1, b2], w=[accb[jj * 16 + h]])

                def bias_finish():
                    hlb = P.buf()
                    P.add("dve", lambda e: e.tensor_scalar(out=acc, in0=acc, scalar1=float(128.0 ** 0.5), scalar2=None,
                                                           op0=ALU.mult), r=accb, w=accb)
                    P.add("dve", lambda e: e.tensor_copy(out=hi_, in_=acc), r=accb, w=[hlb])
                    P.add("dve", lambda e: e.tensor_tensor(out=lo_, in0=acc, in1=hi_, op=ALU.subtract),
                          r=accb + [hlb], w=[hlb])
                    P.add("sp", lambda e: e.dma_start(out=biasd[:, 0:6144], in_=hi_), r=[hlb], dsem=d3)
                    P.add("sp", lambda e: e.dma_start(out=biasd[:, 6144:12288], in_=lo_), r=[hlb], dsem=d3)

                bias_parts = [lambda: bias_part(0), lambda: bias_part(1), lambda: bias_part(2), bias_finish]

            NST = 2
            sf = [AR.f32(8192) for _ in range(NST)]
            sb = [AR.bf16(8192) for _ in range(NST)]
            bf_ = [P.buf() for _ in range(NST)]
            bb_ = [P.buf() for _ in range(NST)]
            dl = [P.dsem() for _ in range(NST)]
            dst_ = [P.dsem() for _ in range(NST)]
            steps = steps_for(layers[0][0], layers[0][1], False) + steps_for("F", 0, False)
            every = max(1, len(steps) // 4)
            for k, stp in enumerate(steps):
                if k % every == 0 and bias_parts:
                    bias_parts.pop(0)()
                i = k % NST
                conv_load(stp, sf[i], bf_[i], dl[i])
                conv_cast_store(stp, sf[i], sb[i], bf_[i], bb_[i], dst_[i], "act")
            while bias_parts:
                bias_parts.pop(0)()
            P.barrier()

        prologue()

        def norm_a_steps(xsrc, t, st, jl=0, jh=T):
            c0 = t * T + 1
            xr, xrb, xrd, sq, sqb = st["xr"], st["xrb"], st["xrd"], st["sq"], st["sqb"]
            acc, accb = st["acc"], st["accb"]
            steps = []
            for kc in range(KC):
                box = {}

                def load(kc=kc, box=box):
                    i = st["n"] % len(xr)
                    st["n"] += 1
                    box["i"] = i
                    P.add("pool", lambda e: e.dma_start(
                        out=xr[i][:, jl:jh], in_=xsrc[kc * 128:(kc + 1) * 128, c0 + jl:c0 + jh]), w=[xrb[i]], dsem=xrd[i])

                def sqf(kc=kc, box=box):
                    i = box["i"]
                    k2 = kc % 2
                    if kc < 2:
                        P.add("act", lambda e: e.activation(out=acc[k2][:, jl:jh], in_=xr[i][:, jl:jh], func=AF.Square),
                              r=[xrb[i]], w=[accb[k2]])
                    else:
                        P.add("act", lambda e: e.activation(out=sq[k2][:, jl:jh], in_=xr[i][:, jl:jh], func=AF.Square),
                              r=[xrb[i]], w=[sqb[k2]])
                        P.add("dve", lambda e: e.tensor_tensor(out=acc[k2][:, jl:jh], in0=acc[k2][:, jl:jh],
                                                               in1=sq[k2][:, jl:jh], op=ALU.add),
                              r=[sqb[k2], accb[k2]], w=[accb[k2]])
                steps.append((load, sqf))
            return steps

        def norm_a(xsrc, t, st, jl=0, jh=T):
            for ld_, sq_ in norm_a_steps(xsrc, t, st, jl, jh):
                ld_()
                sq_()

        def norm_b_steps(xsrc, t, gi, hT, hT_b, st, msb=7, jl=0, jh=T):
            c0 = t * T + 1
            xr, xrb, xrd, rstd, rstdb = st["xr"], st["xrb"], st["xrd"], st["rstd"], st["rstdb"]
            acc, accb = st["acc"], st["accb"]
            ms = bank(msb)

            def head():
                for k2 in range(2):
                    P.add("pe", lambda e, k2=k2: e.matmul(ms[:, jl:jh], lhsT=ones_f, rhs=acc[k2][:, jl:jh],
                                                         start=(k2 == 0), stop=(k2 == 1)),
                          r=[accb[k2], cbuf], w=[pb[msb]])
                P.add("act", lambda e: e.activation(out=rstd[:, jl:jh], in_=ms[:, jl:jh], func=AF.Sqrt, scale=1.0 / D,
                                                    bias=eps_s[:, 0:1]), r=[pb[msb], cbuf], w=[rstdb])
                P.add("dve", lambda e: e.reciprocal(out=rstd[:, jl:jh], in_=rstd[:, jl:jh]), r=[rstdb], w=[rstdb])

            steps = []
            for kc in range(KC):
                box = {}

                def load(kc=kc, box=box):
                    i = st["n"] % len(xr)
                    st["n"] += 1
                    box["i"] = i
                    P.add("pool", lambda e: e.dma_start(
                        out=xr[i][:, jl:jh], in_=xsrc[kc * 128:(kc + 1) * 128, c0 + jl:c0 + jh]), w=[xrb[i]], dsem=xrd[i])

                def apply(kc=kc, box=box):
                    i = box["i"]
                    P.add("dve", lambda e: e.scalar_tensor_tensor(
                        out=hT[:, kc * T + jl:kc * T + jh], in0=xr[i][:, jl:jh],
                        scalar=gn_s[:, gi * KC + kc:gi * KC + kc + 1], in1=rstd[:, jl:jh],
                        op0=ALU.mult, op1=ALU.mult), r=[xrb[i], rstdb, cbuf], w=[hT_b[kc]])
                steps.append((load, apply))
            return head, steps

        def norm_b(xsrc, t, gi, hT, hT_b, st, msb=7, jl=0, jh=T):
            head, steps = norm_b_steps(xsrc, t, gi, hT, hT_b, st, msb, jl, jh)
            head()
            for ld_, ap_ in steps:
                ld_()
                ap_()

        class Spread:
            def __init__(self, hs, h0=0, lag=2):
                self.head, self.steps = hs if hs else (None, [])
                self.h0, self.lag, self.k = h0, lag, 0
                self.nl, self.na = 0, 0

            def tick(self):
                if self.head is None:
                    return
                k = self.k
                self.k += 1
                if k == self.h0:
                    self.head()
                if k > self.h0 and self.nl < KC:
                    self.steps[self.nl][0]()
                    self.nl += 1
                if k > self.h0 + self.lag and self.na < self.nl:
                    self.steps[self.na][1]()
                    self.na += 1

            def flush(self):
                if self.head is None:
                    return
                if self.k <= self.h0:
                    self.head()
                    self.k = self.h0 + 1
                while self.na < KC:
                    if self.nl < KC:
                        self.steps[self.nl][0]()
                        self.nl += 1
                    if self.na < self.nl and (self.nl - self.na > self.lag or self.nl == KC):
                        self.steps[self.na][1]()
                        self.na += 1

        def norm_state(NX=4):
            return dict(xr=[AR.f32(T) for _ in range(NX)], xrb=[P.buf() for _ in range(NX)],
                        xrd=[P.dsem() for _ in range(NX)],
                        sq=[AR.f32(T) for _ in range(2)], sqb=[P.buf() for _ in range(2)],
                        acc=[AR.f32(T) for _ in range(2)], accb=[P.buf() for _ in range(2)],
                        rstd=AR.f32(T), rstdb=P.buf(), n=0)

        def resid_state(NR=3):
            return dict(xr=[AR.f32(T) for _ in range(NR)], xb=[P.buf() for _ in range(NR)],
                        xd=[P.dsem() for _ in range(NR)], n=0)

        def resid_store(rs, xsrc, xdst, dc, t, obank, ml=0, mh=T):
            i = rs["n"] % len(rs["xr"])
            rs["n"] += 1
            xr_, xb_, xd_ = rs["xr"][i], rs["xb"][i], rs["xd"][i]
            c0 = rs["c0"](t)
            P.add("pool", lambda e: e.dma_start(out=xr_[:, ml:mh], in_=xsrc[dc * 128:(dc + 1) * 128, c0 + ml:c0 + mh]),
                  w=[xb_], dsem=xd_)
            P.add("dve", lambda e: e.tensor_tensor(out=xr_[:, ml:mh], in0=bank(obank)[:, ml:mh], in1=xr_[:, ml:mh], op=ALU.add),
                  r=[pb[obank], xb_], w=[xb_])
            P.add("pool", lambda e: e.dma_start(out=xdst[dc * 128:(dc + 1) * 128, c0 + ml:c0 + mh], in_=xr_[:, ml:mh]),
                  r=[xb_], dsem=xd_)

        def wring(n, words):
            return dict(t=[AR.bf16(words) for _ in range(n)], b=[P.buf() for _ in range(n)],
                        d=[P.dsem() for _ in range(n)], n=0)

        def wload(wr, src):
            i = wr["n"] % len(wr["t"])
            wr["n"] += 1
            tl, b, d = wr["t"][i], wr["b"][i], wr["d"][i]
            n = src.shape[-1]
            P.add("sp", lambda e: e.dma_start(out=tl[:, 0:n], in_=src), w=[b], dsem=d)
            return tl, b

        def ffn(l, xsrc, xdst, bg_steps=(), ranges=None):
            AR.reset()
            NBG = 3
            bgf = [AR.f32(2048) for _ in range(NBG)]
            bgb = [AR.bf16(2048) for _ in range(NBG)]
            bgfb = [P.buf() for _ in range(NBG)]
            bgbb = [P.buf() for _ in range(NBG)]
            bgdl = [P.dsem() for _ in range(NBG)]
            bgds = [P.dsem() for _ in range(NBG)]
            bgq = list(bg_steps)
            bgk = [0]
            pend = []

            def bg_tick():
                if pend:
                    stp, i = pend.pop()
                    conv_cast_store(stp, bgf[i], bgb[i], bgfb[i], bgbb[i], bgds[i], "act")
                if bgk[0] < len(bgq):
                    i = bgk[0] % NBG
                    stp = bgq[bgk[0]]
                    bgk[0] += 1
                    conv_load(stp, bgf[i], bgfb[i], bgdl[i])
                    pend.append((stp, i))

            hT2 = [AR.bf16(KC * T) for _ in range(2)]
            hT2_b = [[P.buf() for _ in range(KC)] for _ in range(2)]
            a = AR.bf16(NFC * T)
            a_b = [P.buf() for _ in range(NFC)]
            w1 = wring(3, KC * 128)
            w2 = wring(2, NFC * 128)
            st = norm_state(3)
            rs = resid_state()
            rs["c0"] = lambda t: t * T
            tg = [AR.f32(T) for _ in range(2)]
            tu = [AR.f32(T) for _ in range(2)]
            tgb = [P.buf() for _ in range(2)]
            tub = [P.buf() for _ in range(2)]
            zs = AR.f32(88 * 2)
            zsb = [P.buf() for _ in range(88)]
            wm = AR.f32(88 * 2)
            wmb = P.buf()
            P.add("dve", lambda e: e.memset(zs, 0.0), w=zsb)
            cwl = cw_s[:, l * 352:(l + 1) * 352]

            def cwc(ch, k):
                return cwl[:, ch * 4 + k:ch * 4 + k + 1]

            rg = ranges if ranges is not None else [("full", 0, T)] * NT

            def nrange(t):
                kind, jl, jh = rg[t]
                if kind == "zonly":
                    return max(0, jl - 126) // 128 * 128, jh
                return jl, jh

            norm_a(xsrc, 0, st, *nrange(0))
            norm_b(xsrc, 0, 4 + l, hT2[0], hT2_b[0], st, 7, *nrange(0))
            for t in range(NT):
                hT, hT_b = hT2[t % 2], hT2_b[t % 2]
                kind, jl, jh = rg[t]
                zonly = (kind == "zonly")
                ml, mh = (0 if jl == 0 else jl + 2), jh
                cw3 = cwl.rearrange("p (c k) -> p c k", k=4)
                wm3 = wm.rearrange("p (c k) -> p c k", k=2)
                P.add("dve", lambda e, t=t: e.tensor_scalar(out=wm3[:, :, 0:1], in0=cw3[:, :, 0:1],
                                                          scalar1=hm_s[:, t:t + 1], scalar2=None, op0=ALU.mult),
                      r=[cbuf], w=[wmb])
                P.add("dve", lambda e, t=t: e.tensor_scalar(out=wm3[:, :, 1:2], in0=cw3[:, :, 2:3],
                                                          scalar1=hm_s[:, t:t + 1], scalar2=None, op0=ALU.mult),
                      r=[cbuf], w=[wmb])
                for fc in range(NFC):
                    par = fc % 2
                    zb = {}
                    if fc % 2 == 0:
                        bg_tick()
                    for gu, Wd in ((0, W1G), (1, W1U)):
                        wt, wb_ = wload(w1, Wd[l, fc])
                        bk = par * 2 + gu
                        zb[gu] = bk
                        for kc in range(KC):
                            P.add("pe", lambda e, wt=wt, kc=kc, bk=bk, hT=hT, jl=jl, jh=jh: e.matmul(
                                bank(bk)[:, jl:jh], lhsT=wt[:, kc * 128:(kc + 1) * 128],
                                rhs=hT[:, kc * T + jl:kc * T + jh],
                                start=(kc == 0), stop=(kc == KC - 1)), r=[wb_, hT_b[kc]], w=[pb[bk]])
                    for gu in (0, 1):
                        ch = gu * NFC + fc
                        Pz = bank(zb[gu])
                        pbk = pb[zb[gu]]
                        tt = (tg if gu == 0 else tu)[par]
                        ttb = (tgb if gu == 0 else tub)[par]
                        s0 = zs[:, ch * 2:ch * 2 + 1]
                        s1 = zs[:, ch * 2 + 1:ch * 2 + 2]
                        w0, w1_, w2_, bb = cwc(ch, 0), cwc(ch, 1), cwc(ch, 2), cwc(ch, 3)
                        w0m = wm[:, ch * 2:ch * 2 + 1]
                        w2m = wm[:, ch * 2 + 1:ch * 2 + 2]
                        if not zonly:
                            P.add("act", lambda e, tt=tt, Pz=Pz, w1_=w1_, bb=bb, jl=jl, jh=jh: e.activation(
                                out=tt[:, jl + 1:jh], in_=Pz[:, jl:jh - 1], func=AF.Identity, scale=w1_, bias=bb),
                                r=[pbk, cbuf], w=[ttb])
                            if jl == 0:
                                P.add("act", lambda e, tt=tt, s1=s1, w1_=w1_, bb=bb: e.activation(
                                    out=tt[:, 0:1], in_=s1, func=AF.Identity, scale=w1_, bias=bb),
                                    r=[zsb[ch], cbuf], w=[ttb])
                            P.add("dve", lambda e, tt=tt, Pz=Pz, w0=w0, jl=jl, jh=jh: e.scalar_tensor_tensor(
                                out=tt[:, jl + 2:jh], in0=Pz[:, jl:jh - 2], scalar=w0, in1=tt[:, jl + 2:jh],
                                op0=ALU.mult, op1=ALU.add), r=[pbk, cbuf], w=[ttb])
                            P.add("dve", lambda e, tt=tt, Pz=Pz, w2_=w2_, jl=jl, jh=jh: e.scalar_tensor_tensor(
                                out=tt[:, jl + 1:jh], in0=Pz[:, jl + 1:jh], scalar=w2_, in1=tt[:, jl + 1:jh],
                                op0=ALU.mult, op1=ALU.add), r=[pbk, cbuf], w=[ttb])
                            if jl == 0:
                                P.add("dve", lambda e, tt=tt, s0=s0, w0=w0: e.scalar_tensor_tensor(
                                    out=tt[:, 0:1], in0=s0, scalar=w0, in1=tt[:, 0:1],
                                    op0=ALU.mult, op1=ALU.add), r=[zsb[ch], cbuf], w=[ttb])
                                P.add("dve", lambda e, tt=tt, Pz=Pz, w2m=w2m: e.scalar_tensor_tensor(
                                    out=tt[:, 0:1], in0=Pz[:, 0:1], scalar=w2m, in1=tt[:, 0:1],
                                    op0=ALU.mult, op1=ALU.add), r=[pbk, wmb], w=[ttb])
                                P.add("dve", lambda e, tt=tt, s1=s1, w0m=w0m: e.scalar_tensor_tensor(
                                    out=tt[:, 1:2], in0=s1, scalar=w0m, in1=tt[:, 1:2],
                                    op0=ALU.mult, op1=ALU.add), r=[zsb[ch], wmb], w=[ttb])
                        if jh == T:
                            P.add("dve", lambda e, Pz=Pz, ch=ch: e.tensor_copy(
                                out=zs[:, ch * 2:ch * 2 + 2], in_=Pz[:, T - 2:T]), r=[pbk], w=[zsb[ch]])
                    if zonly:
                        continue
                    P.add("act", lambda e, par=par, ml=ml, mh=mh: e.activation(
                        out=tg[par][:, ml:mh], in_=tg[par][:, ml:mh], func=AF.Silu),
                        r=[tgb[par]], w=[tgb[par]])
                    P.add("dve", lambda e, par=par, fc=fc, ml=ml, mh=mh: e.tensor_tensor(
                        out=a[:, fc * T + ml:fc * T + mh], in0=tg[par][:, ml:mh], in1=tu[par][:, ml:mh], op=ALU.mult),
                        r=[tgb[par], tub[par]], w=[a_b[fc]])
                if t + 1 < NT:
                    norm_a(xsrc, t + 1, st, *nrange(t + 1))
                spr = Spread(norm_b_steps(xsrc, t + 1, 4 + l, hT2[(t + 1) % 2], hT2_b[(t + 1) % 2], st, 7,
                                          *nrange(t + 1)) if t + 1 < NT else None)
                for dc in range(KC):
                    spr.tick()
                    if dc % 2 == 0:
                        bg_tick()
                    if zonly:
                        continue
                    wt, wb_ = wload(w2, W2[l, dc])
                    ob = 4 + dc % 2
                    for fc in range(NFC):
                        P.add("pe", lambda e, wt=wt, fc=fc, ob=ob, ml=ml, mh=mh: e.matmul(
                            bank(ob)[:, ml:mh], lhsT=wt[:, fc * 128:(fc + 1) * 128], rhs=a[:, fc * T + ml:fc * T + mh],
                            start=(fc == 0), stop=(fc == NFC - 1)), r=[wb_, a_b[fc]], w=[pb[ob]])
                    resid_store(rs, xsrc, xdst, dc, t, ob, ml, mh)
                spr.flush()
            while pend or bgk[0] < len(bgq):
                bg_tick()
            P.barrier()

        def mixer_a(j, gi, xsrc, xdst, blocks=None):
            AR.reset()
            hT2 = [AR.bf16(KC * T) for _ in range(2)]
            hT2_b = [[P.buf() for _ in range(KC)] for _ in range(2)]
            vn = AR.bf16(4 * D)
            vn_b = [P.buf() for _ in range(4)]
            yT = AR.bf16(KC * T)
            yT_b = [P.buf() for _ in range(KC)]
            wu = wring(3, KC * 128)
            wv = wring(3, KC * 256)
            st = norm_state()
            rs = resid_state()
            rs["c0"] = lambda t: t * T + 1
            bvb = AR.f32(D)
            vnb = AR.f32(D)
            bsb = AR.f32(1024)
            wsf = AR.f32(1024)
            wsb = AR.bf16(1024)
            lb = P.buf()
            ld = P.dsem()
            P.add("sp", lambda e: e.dma_start(out=bvb, in_=abv[j]), w=[lb], dsem=ld)
            P.add("sp", lambda e: e.dma_start(out=vnb, in_=avn[j]), w=[lb], dsem=ld)
            P.add("sp", lambda e: e.dma_start(out=bsb, in_=absr[j]), w=[lb], dsem=ld)
            P.add("sp", lambda e: e.dma_start(out=wsf, in_=wst[j]), w=[lb], dsem=ld)
            P.add("dve", lambda e: e.tensor_copy(out=wsb, in_=wsf), r=[lb], w=[lb])
            vg4 = [yT.bitcast(F32)[:, 0:D], yT.bitcast(F32)[:, D:2 * D], AR.f32(D), AR.f32(D)]
            vg4b = [[P.buf()] + yT_b[0:8], [P.buf()] + yT_b[8:16], [P.buf()], [P.buf()]]
            ssq = AR.f32(16)
            ssb = P.buf()
            uc_ = [AR.f32(T) for _ in range(2)]
            ucb = [P.buf() for _ in range(2)]
            tmp = [AR.f32(T) for _ in range(2)]
            tmb = [P.buf() for _ in range(2)]
            junk = AR.bf16(D)
            jb = P.buf()
            norm_a(xsrc, 0, st)
            norm_b(xsrc, 0, gi, hT2[0], hT2_b[0], st)
            for t in range(NT):
                hT, hT_b = hT2[t % 2], hT2_b[t % 2]
                bl = (blocks or {}).get(t, [0, 1, 2, 3])
                cl, ch_ = bl[0] * 128, (bl[-1] + 1) * 128
                for cg in range(8):
                    wt, wb_ = wload(wv, AV[j, cg])
                    for b in bl:
                        bk = (cg * 4 + b) % 2
                        for kc in range(KC):
                            P.add("pe", lambda e, wt=wt, kc=kc, bk=bk, b=b, hT=hT: e.matmul(
                                bank(bk)[:, 0:256], lhsT=hT[:, kc * T + b * 128:kc * T + (b + 1) * 128],
                                rhs=wt[:, kc * 256:(kc + 1) * 256],
                                start=(kc == 0), stop=(kc == KC - 1)), r=[wb_, hT_b[kc]], w=[pb[bk]])
                        P.add("dve", lambda e, bk=bk, cg=cg, b=b: e.tensor_tensor(
                            out=vg4[b][:, cg * 256:(cg + 1) * 256], in0=bank(bk)[:, 0:256],
                            in1=bvb[:, cg * 256:(cg + 1) * 256], op=ALU.add),
                            r=[pb[bk], lb], w=vg4b[b])
                nsteps = norm_a_steps(xsrc, t + 1, st) if t + 1 < NT else []
                for b in bl:
                    P.add("act", lambda e, b=b: e.activation(out=vg4[b], in_=vg4[b], func=AF.Gelu_apprx_tanh),
                          r=vg4b[b], w=vg4b[b])
                    P.add("act", lambda e, b=b: e.activation(out=junk, in_=vg4[b], func=AF.Square,
                                                             accum_out=ssq[:, b:b + 1]),
                          r=vg4b[b], w=[jb, ssb])
                P.add("act", lambda e: e.activation(out=ssq[:, 4:8], in_=ssq[:, 0:4], func=AF.Sqrt,
                                                    scale=1.0 / D, bias=eps_s[:, 0:1]), r=[ssb, cbuf], w=[ssb])
                P.add("dve", lambda e: e.reciprocal(out=ssq[:, 8:12], in_=ssq[:, 4:8]), r=[ssb], w=[ssb])
                for b in bl:
                    P.add("dve", lambda e, b=b: e.scalar_tensor_tensor(
                        out=vn[:, b * D:(b + 1) * D], in0=vg4[b], scalar=ssq[:, 8 + b:9 + b], in1=vnb,
                        op0=ALU.mult, op1=ALU.mult), r=vg4b[b] + [ssb, lb], w=[vn_b[b]])
                for cc in range(KC):
                    wt, wb_ = wload(wu, AU[j, cc])
                    ub = 2 + cc % 2
                    sbk = 4 + cc % 2
                    p2 = cc % 2
                    if nsteps:
                        nsteps[cc][0]()
                        if cc >= 1:
                            nsteps[cc - 1][1]()
                    for kc in range(KC):
                        P.add("pe", lambda e, wt=wt, kc=kc, ub=ub, hT=hT, cl=cl, ch_=ch_: e.matmul(
                            bank(ub)[:, cl:ch_], lhsT=wt[:, kc * 128:(kc + 1) * 128], rhs=hT[:, kc * T + cl:kc * T + ch_],
                            start=(kc == 0), stop=(kc == KC - 1)), r=[wb_, hT_b[kc]], w=[pb[ub]])
                    g = cc // 2
                    for b in bl:
                        P.add("pe", lambda e, b=b, cc=cc, g=g, sbk=sbk: e.matmul(
                            bank(sbk)[:, b * 128:(b + 1) * 128],
                            lhsT=vn[:, b * D + cc * 128:b * D + (cc + 1) * 128],
                            rhs=wsb[:, g * 128:(g + 1) * 128], start=True, stop=True),
                            r=[vn_b[b], lb], w=[pb[sbk]])
                    P.add("act", lambda e, ub=ub, p2=p2, cc=cc, cl=cl, ch_=ch_: e.activation(
                        out=uc_[p2][:, cl:ch_], in_=bank(ub)[:, cl:ch_], func=AF.Gelu_apprx_tanh,
                        bias=abu_s[:, j * KC + cc:j * KC + cc + 1]), r=[pb[ub], cbuf], w=[ucb[p2]])
                    for b in bl:
                        P.add("dve", lambda e, b=b, g=g, sbk=sbk, p2=p2: e.tensor_tensor(
                            out=tmp[p2][:, b * 128:(b + 1) * 128], in0=bank(sbk)[:, b * 128:(b + 1) * 128],
                            in1=bsb[:, g * 128:(g + 1) * 128], op=ALU.add), r=[pb[sbk], lb], w=[tmb[p2]])
                    P.add("dve", lambda e, p2=p2, cc=cc, cl=cl, ch_=ch_: e.tensor_tensor(
                        out=yT[:, cc * T + cl:cc * T + ch_], in0=tmp[p2][:, cl:ch_], in1=uc_[p2][:, cl:ch_], op=ALU.mult),
                        r=[tmb[p2], ucb[p2]], w=[yT_b[cc]])
                if nsteps:
                    nsteps[KC - 1][1]()
                spr = Spread(norm_b_steps(xsrc, t + 1, gi, hT2[(t + 1) % 2], hT2_b[(t + 1) % 2], st)
                             if t + 1 < NT else None, h0=1)
                for dc in range(KC):
                    spr.tick()
                    wt, wb_ = wload(wu, AO[j, dc])
                    ob = 6 if dc % 2 == 0 else 0
                    for kc in range(KC):
                        P.add("pe", lambda e, wt=wt, kc=kc, ob=ob, cl=cl, ch_=ch_: e.matmul(
                            bank(ob)[:, cl:ch_], lhsT=wt[:, kc * 128:(kc + 1) * 128], rhs=yT[:, kc * T + cl:kc * T + ch_],
                            start=(kc == 0), stop=(kc == KC - 1)), r=[wb_, yT_b[kc]], w=[pb[ob]])
                    resid_store(rs, xsrc, xdst, dc, t, ob, cl, ch_)
                spr.flush()
            P.barrier()

        def mixer_b(j, gi, xsrc, xdst, qblocks=None):
            AR.reset()
            RT = 3
            RB = RT * 4
            hT2 = [AR.bf16(KC * T) for _ in range(2)]
            hT2_b = [[P.buf() for _ in range(KC)] for _ in range(2)]
            KT = AR.bf16(4 * RB * 128)
            KT_b = [[P.buf() for _ in range(4)] for _ in range(RT)]
            Vt = AR.bf16(RB * 512)
            V_b = [P.buf() for _ in range(RB)]
            qT = AR.bf16(KC * T)
            qT_b = [P.buf() for _ in range(KC)]
            oall = AR.bf16(KC * T)
            oall_b = [P.buf() for _ in range(KC)]
            wq = wring(3, KC * 128)
            wvr = wring(1, KC * 512)
            st = norm_state(3)
            rs = resid_state(2)
            rs["c0"] = lambda t: t * T + 1
            bT = AR.bf16(2 * 3 * 16 * 128)
            idf = AR.f32(128)
            idb = AR.bf16(128)
            es_ = AR.f32(16)
            lb = P.buf()
            ld = P.dsem()
            P.add("sp", lambda e: e.dma_start(out=bT, in_=biasd), w=[lb], dsem=ld)
            P.add("sp", lambda e: e.dma_start(out=es_, in_=sink[j]), w=[lb], dsem=ld)
            P.add("act", lambda e: e.activation(out=es_, in_=es_, func=AF.Exp), r=[lb], w=[lb])
            P.add("sp", lambda e: e.dma_start(out=idf, in_=identd), w=[lb], dsem=ld)
            P.add("dve", lambda e: e.tensor_copy(out=idb, in_=idf), r=[lb], w=[lb])
            NE = 8
            pT = [AR.bf16(T) for _ in range(NE)]
            pTb = [P.buf() for _ in range(NE)]
            ds_ = [AR.f32(T) for _ in range(2)]
            dsb = [P.buf() for _ in range(2)]
            vt, vtb = wload(wvr, BV[j, 0])
            scale = 128.0 ** -0.5
            cnt = {"u": 0, "g": 0}

            def rp(kb):
                return ((kb // 4) % RT) * 4 + kb % 4

            def phase1(t):
                hp = t % 2
                slot = t % RT
                for kh in range(4):
                    wt, wb_ = wload(wq, BK[j, kh])
                    bk = kh % 2
                    for kc in range(KC):
                        P.add("pe", lambda e, wt=wt, kc=kc, bk=bk, hp=hp: e.matmul(
                            bank(bk), lhsT=wt[:, kc * 128:(kc + 1) * 128], rhs=hT2[hp][:, kc * T:(kc + 1) * T],
                            start=(kc == 0), stop=(kc == KC - 1)), r=[wb_, hT2_b[hp][kc]], w=[pb[bk]])
                    o_ = KT[:, kh * RB * 128 + slot * 512:kh * RB * 128 + (slot + 1) * 512]
                    P.add("act", lambda e, o_=o_, bk=bk: e.copy(out=o_, in_=bank(bk)), r=[pb[bk]], w=[KT_b[slot][kh]])
                for b in range(4):
                    bk = 2 + b % 2
                    gb = rp(t * 4 + b)
                    for kc in range(KC):
                        P.add("pe", lambda e, kc=kc, bk=bk, b=b, hp=hp: e.matmul(
                            bank(bk), lhsT=hT2[hp][:, kc * T + b * 128:kc * T + (b + 1) * 128],
                            rhs=vt[:, kc * 512:(kc + 1) * 512], start=(kc == 0), stop=(kc == KC - 1)),
                            r=[vtb, hT2_b[hp][kc]], w=[pb[bk]])
                    P.add("dve", lambda e, gb=gb, bk=bk: e.tensor_copy(out=Vt[:, gb * 512:(gb + 1) * 512], in_=bank(bk)),
                          r=[pb[bk]], w=[V_b[gb]])

            def logits(u):
                b, kh, i, jj, n_, nj, g = u["b"], u["kh"], u["i"], u["jj"], u["n"], u["nj"], u["g"]
                s_ = cnt["u"]
                cnt["u"] += 1
                u["s"] = s_
                lbk, ei = s_ % 4, s_ % NE
                kb = i - 1 + jj
                kslot = (kb // 4) % RT
                kcol = kh * RB * 128 + rp(kb) * 128
                q3 = qT.rearrange("p (h n) -> p h n", h=KC)[:, kh * 4:(kh + 1) * 4, b * 128:(b + 1) * 128]
                qr = [qT_b[kh * 4 + hh] for hh in range(4)]
                o3_ = bank(lbk).rearrange("p (h n) -> p h n", h=4)
                P.add("pe", lambda e: e.matmul(o3_, lhsT=KT[:, kcol:kcol + 128], rhs=q3,
                                               start=True, stop=False), r=[KT_b[kslot][kh]] + qr, w=[pb[lbk]])
                for hl in range(2):
                    b3 = bT.rearrange("p (s j h n) -> p s j h n", s=2, j=3, h=16)[:, hl, jj, kh * 4:(kh + 1) * 4, :]
                    P.add("pe", lambda e, b3=b3, hl=hl: e.matmul(o3_, lhsT=idb, rhs=b3, start=False, stop=(hl == 1)),
                          r=[lb], w=[pb[lbk]])
                P.add("act", lambda e: e.activation(
                    out=pT[ei], in_=bank(lbk), func=AF.Exp, scale=scale, bias=mv_s[:, i * 3 + jj:i * 3 + jj + 1]),
                    r=[pb[lbk], cbuf], w=[pTb[ei]])

            def pvden(u, t):
                b, kh, i, jj, n_, nj, g = u["b"], u["kh"], u["i"], u["jj"], u["n"], u["nj"], u["g"]
                ei = u["s"] % NE
                ob = 4 + 2 * (g % 2)
                db = ob + 1
                di = g % 2
                vb_ = rp(i - 1 + jj)
                P.add("pe", lambda e: e.matmul(
                    bank(ob), lhsT=Vt[:, vb_ * 512 + kh * 128:vb_ * 512 + (kh + 1) * 128], rhs=pT[ei],
                    start=(n_ == 0), stop=(n_ == nj - 1)), r=[V_b[vb_], pTb[ei]], w=[pb[ob]])
                P.add("pe", lambda e: e.matmul(
                    bank(db), lhsT=ones_b, rhs=pT[ei],
                    start=(n_ == 0), stop=(n_ == nj - 1)), r=[cbuf, pTb[ei]], w=[pb[db]])
                if n_ != nj - 1:
                    return
                for hh in range(4):
                    h = kh * 4 + hh
                    P.add("act", lambda e, hh=hh, h=h: e.activation(
                        out=ds_[di][:, hh * 128:(hh + 1) * 128], in_=bank(db)[:, hh * 128:(hh + 1) * 128],
                        func=AF.Ln, bias=es_[:, h:h + 1]), r=[pb[db], lb], w=[dsb[di]])
                P.add("act", lambda e: e.activation(out=ds_[di], in_=ds_[di], func=AF.Exp, scale=-1.0),
                      r=[dsb[di]], w=[dsb[di]])
                o3 = oall.rearrange("p (h n) -> p h n", h=KC)[:, kh * 4:(kh + 1) * 4, b * 128:(b + 1) * 128]
                P.add("dve", lambda e: e.tensor_tensor(
                    out=o3, in0=bank(ob).rearrange("p (h n) -> p h n", h=4),
                    in1=ds_[di].rearrange("p (h n) -> p h n", h=4), op=ALU.mult),
                    r=[pb[ob], dsb[di]], w=[oall_b[kh * 4 + hh] for hh in range(4)])

            def phase2(t):
                hp = t % 2
                bl = (qblocks or {}).get(t, [0, 1, 2, 3])
                cl, ch_ = bl[0] * 128, (bl[-1] + 1) * 128
                for hq in range(KC):
                    wt, wb_ = wload(wq, BQ[j, hq])
                    bk = hq % 2
                    for kc in range(KC):
                        P.add("pe", lambda e, wt=wt, kc=kc, bk=bk, hp=hp: e.matmul(
                            bank(bk)[:, cl:ch_], lhsT=wt[:, kc * 128:(kc + 1) * 128],
                            rhs=hT2[hp][:, kc * T + cl:kc * T + ch_],
                            start=(kc == 0), stop=(kc == KC - 1)), r=[wb_, hT2_b[hp][kc]], w=[pb[bk]])
                    P.add("act", lambda e, hq=hq, bk=bk: e.copy(out=qT[:, hq * T + cl:hq * T + ch_], in_=bank(bk)[:, cl:ch_]),
                          r=[pb[bk]], w=[qT_b[hq]])
                nsteps = norm_a_steps(xsrc, t + 2, st) if t + 2 < NT else []
                units = []
                for b in bl:
                    i = t * 4 + b
                    for kh in range(4):
                        js = [jj for jj in range(3) if 0 <= i - 1 + jj < NBLK]
                        g = cnt["g"]
                        cnt["g"] += 1
                        for n_, jj in enumerate(js):
                            units.append(dict(b=b, kh=kh, i=i, jj=jj, n=n_, nj=len(js), g=g))
                SK = 4
                spr = Spread(norm_b_steps(xsrc, t + 2, gi, hT2[hp], hT2_b[hp], st, msb=3)
                             if t + 2 < NT else None, h0=KC + 2)
                for k in range(max(len(units) + SK, KC + 2 if nsteps else 0)):
                    if nsteps and k < KC:
                        nsteps[k][0]()
                    if nsteps and 2 <= k < KC + 2:
                        nsteps[k - 2][1]()
                    spr.tick()
                    if k < len(units):
                        logits(units[k])
                    if 0 <= k - SK < len(units):
                        pvden(units[k - SK], t)
                spr.flush()
                for dc in range(KC):
                    wt, wb_ = wload(wq, BO[j, dc])
                    ob = 2 + dc % 2
                    for kc in range(KC):
                        P.add("pe", lambda e, wt=wt, kc=kc, ob=ob: e.matmul(
                            bank(ob)[:, cl:ch_], lhsT=wt[:, kc * 128:(kc + 1) * 128], rhs=oall[:, kc * T + cl:kc * T + ch_],
                            start=(kc == 0), stop=(kc == KC - 1)), r=[wb_, oall_b[kc]], w=[pb[ob]])
                    resid_store(rs, xsrc, xdst, dc, t, ob, cl, ch_)

            norm_a(xsrc, 0, st)
            norm_b(xsrc, 0, gi, hT2[0], hT2_b[0], st)
            phase1(0)
            if NT > 1:
                norm_a(xsrc, 1, st)
                norm_b(xsrc, 1, gi, hT2[1], hT2_b[1], st)
            for t in range(NT):
                if t + 1 < NT:
                    phase1(t + 1)
                phase2(t)
            P.barrier()

        def final_norm(xsrc):
            AR.reset()
            xt = [AR.f32(KC * T) for _ in range(2)]
            xtb = [P.buf() for _ in range(2)]
            xtd = [P.dsem() for _ in range(2)]
            sq = [AR.f32(T) for _ in range(2)]
            sqb = [P.buf() for _ in range(2)]
            acc = AR.f32(T)
            accb = P.buf()
            rstd = AR.f32(T)
            rstdb = P.buf()
            n = 0
            for t in range(NT):
                lo = max(t * T, out_off)
                hi = min((t + 1) * T, out_off + n_out_tok)
                if lo >= hi:
                    continue
                c0 = t * T + 1
                i = n % 2
                n += 1
                x3 = xt[i].rearrange("p (k n) -> p k n", k=KC)
                src = xsrc[:, c0:c0 + T].rearrange("(k p) n -> p k n", p=128)
                P.add("sp", lambda e, x3=x3, src=src: e.dma_start(out=x3, in_=src), w=[xtb[i]], dsem=xtd[i])
                for kc in range(KC):
                    k2 = kc % 2
                    xk = xt[i][:, kc * T:(kc + 1) * T]
                    if kc == 0:
                        P.add("act", lambda e, xk=xk: e.activation(out=acc, in_=xk, func=AF.Square), r=[xtb[i]], w=[accb])
                    else:
                        P.add("act", lambda e, xk=xk, k2=k2: e.activation(out=sq[k2], in_=xk, func=AF.Square),
                              r=[xtb[i]], w=[sqb[k2]])
                        P.add("dve", lambda e, k2=k2: e.tensor_tensor(out=acc, in0=acc, in1=sq[k2], op=ALU.add),
                              r=[sqb[k2], accb], w=[accb])
                P.add("pe", lambda e: e.matmul(bank(7), lhsT=ones_f, rhs=acc, start=True, stop=True),
                      r=[accb, cbuf], w=[pb[7]])
                P.add("act", lambda e: e.activation(out=rstd, in_=bank(7), func=AF.Sqrt, scale=1.0 / D,
                                                    bias=eps_s[:, 0:1]), r=[pb[7], cbuf], w=[rstdb])
                P.add("dve", lambda e: e.reciprocal(out=rstd, in_=rstd), r=[rstdb], w=[rstdb])
                for kc in range(KC):
                    xk = xt[i][:, kc * T:(kc + 1) * T]
                    P.add("dve", lambda e, xk=xk, kc=kc: e.scalar_tensor_tensor(
                        out=xk, in0=xk, scalar=gn_s[:, 8 * KC + kc:8 * KC + kc + 1], in1=rstd,
                        op0=ALU.mult, op1=ALU.mult), r=[xtb[i], rstdb, cbuf], w=[xtb[i]])
                dst = y_out[:, lo - out_off:hi - out_off].rearrange("(k p) n -> p k n", p=128)
                P.add("sp", lambda e, x3=x3, dst=dst, lo=lo, hi=hi, t=t: e.dma_start(
                    out=dst, in_=x3[:, :, lo - t * T:hi - t * T]), r=[xtb[i]], dsem=xtd[i])
            P.barrier()

        cur, nxt = x_in, xa
        for l, (kind, j) in enumerate(layers):
            msub = None
            if isinstance(trim, dict):
                msub = trim.get(("M", l))
            elif trim and NT == 8 and len(layers) == 4:
                msub = {1: {0: [1, 2, 3], 7: [0, 1, 2]}, 2: {0: [2, 3], 7: [0, 1]}, 3: {0: [3], 7: [0]}}.get(l)
            if kind == "A":
                mixer_a(j, l, cur, nxt, msub)
            else:
                mixer_b(j, l, cur, nxt, msub)
            cur, nxt = nxt, (xb if nxt is xa else xa)
            bg = []
            if l + 1 < len(layers):
                bg = steps_for(layers[l + 1][0], layers[l + 1][1], True) + steps_for("F", l + 1, True)
            rgs = None
            if isinstance(trim, dict):
                rgs = trim.get(l)
            elif trim and NT == 8 and len(layers) == 4:
                lo_hi = {0: (124, 388), 1: (252, 260), 2: (380, 132)}
                if l < 3:
                    jl0, jh7 = lo_hi[l]
                    rgs = [("full", jl0, T)] + [("full", 0, T)] * 6 + [("full", 0, jh7)]
                else:
                    rgs = [("zonly", 510, T)] + [("full", 0, T)] * 6 + [("full", 0, 4)]
            ffn(l, cur, nxt, bg, rgs)
            cur, nxt = nxt, (xb if nxt is xa else xa)
        final_norm(cur)

        P.finalize()
        with nc.Block() as block:
            @block.tensor
            def _(e):
                P.emit("pe", e)

            @block.scalar
            def _(e):
                P.emit("act", e)

            @block.vector
            def _(e):
                P.emit("dve", e)

            @block.gpsimd
            def _(e):
                P.emit("pool", e)

            @block.sync
            def _(e):
                P.emit("sp", e)
    return nc


def _relative_bucket(rel):
    half, max_exact = 16, 8
    ret = (rel > 0).astype(np.int32) * half
    n = np.abs(rel)
    nf = np.maximum(n, 1).astype(np.float32)
    large = max_exact + (np.log(nf / max_exact) / np.log(128 / max_exact) * (half - max_exact)).astype(np.int32)
    large = np.minimum(large, half - 1)
    return (ret + np.where(n < max_exact, n, large)).astype(np.int32)


def static_tables():
    c = np.arange(128)[:, None]
    q = np.arange(128)[None, :]
    bm = np.zeros((3, 128, 32, 128), np.float32)
    wm = np.zeros((128, 3, 128), np.float32)
    for jj in range(3):
        rel = (jj - 1) * 128 + c - q
        bk = _relative_bucket(rel)
        for b in range(32):
            bm[jj, :, b, :] = (bk == b)
        wm[:, jj, :] = np.where(np.abs(rel) <= 128, 0.0, NEG)
    return bm.reshape(3, 128, 32 * 128), wm.reshape(128, 3 * 128)


def per_partition(v):
    v = np.asarray(v, np.float32)
    lead = v.shape[:-1]
    n = v.shape[-1] // 128
    return np.ascontiguousarray(np.moveaxis(v.reshape(*lead, n, 128), -1, 0))


def shared_inputs(inp):
    f = np.float32
    d = {}
    gn = np.concatenate([inp["mix_norm"], inp["ffn_norm"], inp["final_norm"][None]], 0)
    d["gn"] = per_partition(gn).reshape(128, 9 * KC)
    cw = np.concatenate([inp["f_conv_w"], inp["f_conv_b"][:, None, :]], 1)
    cwp = per_partition(cw)
    d["cw"] = np.ascontiguousarray(cwp.transpose(0, 1, 3, 2)).reshape(128, 4 * 88 * 4)
    d["abu"] = per_partition(inp["a_b_in"][:, :D]).reshape(128, 2 * KC)
    d["abv"] = np.ascontiguousarray(np.broadcast_to(inp["a_b_in"][:, None, D:], (2, 128, D))).astype(f)
    d["avn"] = np.ascontiguousarray(np.broadcast_to(inp["a_v_norm"][:, None, :], (2, 128, D))).astype(f)
    d["absr"] = np.ascontiguousarray(np.broadcast_to(inp["a_b_s"].reshape(2, 1, 1024), (2, 128, 1024))).astype(f)
    d["wst"] = np.ascontiguousarray(np.asarray(inp["a_w_s"], f).transpose(0, 3, 1, 2)).reshape(2, 128, 1024)
    d["sink"] = np.ascontiguousarray(np.broadcast_to(np.asarray(inp["b_sink"], f)[:, None, :], (2, 128, 16)))
    d["rbr"] = np.ascontiguousarray(np.broadcast_to(np.asarray(inp["rel_bias"], f).reshape(1, 512), (128, 512)))
    bm, wm = static_tables()
    d["bmask"], d["wmask"] = bm, wm
    d["ident"] = np.eye(128, dtype=np.float32)
    for k in ("a_w_in", "a_w_out", "b_w_qkv", "b_w_out", "f_w_in", "f_w_out"):
        d[k] = np.ascontiguousarray(np.asarray(inp[k], f))
    return d


def window_inputs(xflat_T, start, NT, bounds, ntok):
    W = NT * T
    xw = np.zeros((D, W + 2), np.float32)
    lo, hi = max(start, 0), min(start + W, ntok)
    if hi > lo:
        xw[:, 1 + lo - start:1 + hi - start] = xflat_T[:, lo:hi]
    bset = set(bounds)
    hm = np.ones((NT,), np.float32)
    for t in range(NT):
        if t == 0 or (start + t * T) in bset:
            hm[t] = 0.0
    NBLK = W // 128
    mv = np.zeros((NBLK, 3), np.float32)
    for i in range(NBLK):
        s = start + i * 128
        if s in bset or i == 0:
            mv[i, 0] = NEG
        if (s + 128) in bset or i == NBLK - 1:
            mv[i, 2] = NEG
    return {"x_in": xw,
            "hmask": np.ascontiguousarray(np.broadcast_to(hm[None], (128, NT))),
            "maskv": np.ascontiguousarray(np.broadcast_to(mv.reshape(1, -1), (128, NBLK * 3)))}


_CACHE = {}


def kernel(**inp):
    NT = (OWN + 2 * HALO) // T
    layers = (("A", 0), ("B", 0), ("A", 1), ("B", 1))
    key = "full"
    if key not in _CACHE:
        _CACHE[key] = build(NT, layers, OWN, HALO)
    nc = _CACHE[key]
    xp = np.asarray(inp["x_prompt"], np.float32).reshape(-1, D)
    xs = np.asarray(inp["x_sample"], np.float32).reshape(-1, D)
    xflat_T = np.ascontiguousarray(np.concatenate([xp, xs], 0).T)
    sh = shared_inputs(inp)
    in_maps = []
    for c in range(8):
        m = dict(sh)
        m.update(window_inputs(xflat_T, c * OWN - HALO, NT, SEQ_BOUNDS, NTOK))
        in_maps.append(m)
    res = run_bass_kernel_spmd(nc, in_maps, core_ids=list(range(8)))
    yT = np.concatenate([np.asarray(r["y_out"]) for r in res.results], axis=1)
    y = np.ascontiguousarray(yT.T)
    n_p = xp.shape[0]
    return (y[:n_p].reshape(inp["x_prompt"].shape).astype(np.float32),
            y[n_p:].reshape(inp["x_sample"].shape).astype(np.float32))
```

```python
import numpy as np
import concourse.bass as bass
import concourse.mybir as mybir
from concourse.bass_utils import run_bass_kernel_spmd

F32 = mybir.dt.float32
BF16 = mybir.dt.bfloat16
AF = mybir.ActivationFunctionType
ALU = mybir.AluOpType

D = 2048
KC = 16
DFF = 5632
NFC = 44
T = 512
NEG = -30000.0
EPS = 1e-6
OWN = 3072
HALO = 512
SEQ_BOUNDS = (0, 8192, 16384, 20480, 24576)
NTOK = 24576


class Buf:
    __slots__ = ("name", "w", "r")

    def __init__(self, name=""):
        self.name = name
        self.w = None
        self.r = {}


class Sem:
    def __init__(self, h):
        self.h = h
        self.n = 0


class Op:
    __slots__ = ("eng", "fn", "deps", "inc", "sem", "val", "dma", "extra")


class Prog:
    ENGS = ("pe", "act", "dve", "pool", "sp")

    def __init__(self, esems, dsem_pool):
        self.q = {e: [] for e in self.ENGS}
        self.esem = esems
        self.dsems = [Sem(h) for h in dsem_pool]
        self.dfree = list(self.dsems)
        self.bufs = []
        self.nuid = 0

    def buf(self, name=""):
        b = Buf(name)
        self.bufs.append(b)
        return b

    def dsem(self):
        return self.dfree.pop()

    def add(self, eng, fn, r=(), w=(), dsem=None):
        o = Op()
        o.eng, o.fn, o.inc, o.extra = eng, fn, False, ()
        o.dma = dsem is not None
        deps = set()
        for b in r:
            if b.w is not None:
                deps.add(b.w)
        for b in w:
            if b.w is not None:
                deps.add(b.w)
            deps.update(b.r.values())
        deps.discard(o)
        for d in deps:
            d.inc = True
        o.deps = deps
        self.nuid += 1
        key = ("d", self.nuid) if o.dma else eng
        for b in r:
            b.r[key] = o
        for b in w:
            b.w = o
            b.r = {}
        if o.dma:
            dsem.n += 1
            o.sem, o.val = dsem, 16 * dsem.n
        else:
            o.sem, o.val = None, 0
        self.q[eng].append(o)
        return o

    def barrier(self):
        lasts = []
        for e in self.ENGS:
            for o in reversed(self.q[e]):
                if o.fn is not None:
                    if not o.dma:
                        o.inc = True
                        lasts.append(o)
                    break
        extra = tuple((s, 16 * s.n) for s in self.dsems if s.n > 0)
        for e in self.ENGS:
            o = Op()
            o.eng, o.fn, o.inc, o.dma = e, None, False, False
            o.deps, o.sem, o.val, o.extra = set(lasts), None, 0, extra
            self.q[e].append(o)
        for b in self.bufs:
            b.w = None
            b.r = {}
        self.bufs = []
        self.dfree = list(self.dsems)

    def finalize(self):
        for e in self.ENGS:
            s = self.esem[e]
            for o in self.q[e]:
                if o.inc and not o.dma and o.fn is not None:
                    s.n += 1
                    o.sem, o.val = s, s.n

    def emit(self, name, e):
        seen = {}
        for o in self.q[name]:
            waits = [(d.sem, d.val) for d in o.deps
                     if not (name == "pe" and d.eng == "pe" and not d.dma)]
            waits.extend(o.extra)
            for s, v in sorted(waits, key=lambda t: (id(t[0]), t[1])):
                if s is None:
                    continue
                k = id(s)
                if seen.get(k, 0) >= v:
                    continue
                e.wait_ge(s.h, v)
                seen[k] = v
            if o.fn is None:
                continue
            ins = o.fn(e)
            if o.dma:
                ins.then_inc(o.sem.h, 16)
            elif o.inc:
                ins.then_inc(o.sem.h, 1)


class Arena:
    def __init__(self, ap, nwords):
        self.ap, self.n, self.off = ap, nwords, 0

    def reset(self):
        self.off = 0

    def f32(self, n):
        n8 = (n + 7) // 8 * 8
        assert self.off + n8 <= self.n, ("arena overflow", self.off + n8, self.n)
        a = self.ap[:, self.off:self.off + n]
        self.off += n8
        return a

    def bf16(self, n):
        w = (n + 15) // 16 * 8
        return self.f32(w).bitcast(BF16)[:, 0:n]


def build(NT, layers, n_out_tok, out_off, trim=True):
    W = NT * T
    XW = W + 2
    NBLK = W // 128
    nc = bass.Bass("TRN2", target_bir_lowering=False)

    def din(name, shape, dt=F32):
        return nc.dram_tensor(name, list(shape), dt, kind="ExternalInput").ap()

    def dscr(name, shape, dt=BF16):
        return nc.dram_tensor(name, list(shape), dt, kind="Internal").ap()

    x_in = din("x_in", [D, XW])
    hmask = din("hmask", [128, NT])
    maskv = din("maskv", [128, NBLK * 3])
    gn = din("gn", [128, 9 * KC])
    cwd = din("cw", [128, 4 * 88 * 4])
    abu = din("abu", [128, 2 * KC])
    abv = din("abv", [2, 128, D])
    avn = din("avn", [2, 128, D])
    absr = din("absr", [2, 128, 8 * 128])
    wst = din("wst", [2, 128, 8 * 128])
    sink = din("sink", [2, 128, 16])
    rbr = din("rbr", [128, 32 * 16])
    bmask = din("bmask", [3, 128, 32 * 128])
    wmask = din("wmask", [128, 3 * 128])
    identd = din("ident", [128, 128])
    a_w_in = din("a_w_in", [2, D, 2 * D])
    a_w_out = din("a_w_out", [2, D, D])
    b_w_qkv = din("b_w_qkv", [2, D, 3072])
    b_w_out = din("b_w_out", [2, D, D])
    f_w_in = din("f_w_in", [4, D, 2 * DFF])
    f_w_out = din("f_w_out", [4, DFF, D])
    y_out = nc.dram_tensor("y_out", [D, n_out_tok], F32, kind="ExternalOutput").ap()

    xa = dscr("xa", [D, XW], F32)
    xb = dscr("xb", [D, XW], F32)
    W1G = dscr("W1G", [4, NFC, 128, KC * 128])
    W1U = dscr("W1U", [4, NFC, 128, KC * 128])
    W2 = dscr("W2", [4, KC, 128, NFC * 128])
    AU = dscr("AU", [2, KC, 128, KC * 128])
    AV = dscr("AV", [2, 8, 128, KC * 256])
    AO = dscr("AO", [2, KC, 128, KC * 128])
    BQ = dscr("BQ", [2, KC, 128, KC * 128])
    BK = dscr("BK", [2, 4, 128, KC * 128])
    BV = dscr("BV", [2, 1, 128, KC * 512])
    BO = dscr("BO", [2, KC, 128, KC * 128])
    biasd = dscr("biasd", [128, 2 * 3 * 16 * 128], BF16)

    ARW = 46400
    arena_t = nc.alloc_sbuf_tensor("arena", [128, ARW], F32)
    AR = Arena(arena_t.ap(), ARW)
    cst_t = nc.alloc_sbuf_tensor("cst", [128, 1984], F32)
    CST = Arena(cst_t.ap(), 1984)
    ps_t = nc.alloc_psum_tensor("ps", [128, 8 * 512], F32)
    PS = ps_t.ap()

    def bank(i):
        return PS[:, i * 512:(i + 1) * 512]

    import contextlib
    with contextlib.ExitStack() as es:
        sems = [es.enter_context(nc.semaphore("s%d" % i)) for i in range(45)]
        esems = {e: Sem(sems[i]) for i, e in enumerate(Prog.ENGS)}
        P = Prog(esems, sems[5:])
        pb = [P.buf("bank%d" % i) for i in range(8)]

        gn_s = CST.f32(9 * KC)
        cw_s = CST.f32(4 * 88 * 4)
        abu_s = CST.f32(2 * KC)
        hm_s = CST.f32(NT)
        mv_s = CST.f32(NBLK * 3)
        ones_f = CST.f32(128)
        eps_s = CST.f32(8)
        zero_s = CST.f32(8)
        ones_b = CST.bf16(128)
        cbuf = P.buf("consts")
        cds = P.dsem()
        for dst, src in ((gn_s, gn), (cw_s, cwd), (abu_s, abu), (hm_s, hmask), (mv_s, maskv)):
            P.add("sp", lambda e, dst=dst, src=src: e.dma_start(out=dst, in_=src), w=[cbuf], dsem=cds)
        P.add("dve", lambda e: e.memset(ones_f, 1.0), w=[cbuf])
        P.add("dve", lambda e: e.memset(ones_b, 1.0), w=[cbuf])
        P.add("dve", lambda e: e.memset(eps_s, EPS), w=[cbuf])
        P.add("dve", lambda e: e.memset(zero_s, 0.0), w=[cbuf])
        def steps_for(kind, idx, small):
            out = []

            def add(src2d, col0, Atot, CWd, NB, dst4, ch0):
                if not small:
                    out.append((src2d, col0, 0, Atot, CWd, NB, dst4, ch0))
                else:
                    for a0 in range(0, Atot, 4):
                        out.append((src2d, col0, a0, 4, CWd, NB, dst4, ch0))

            if kind == "F":
                l = idx
                for c4 in range(NFC // 4):
                    add(f_w_in[l], c4 * 512, KC, 128, 4, W1G[l], c4 * 4)
                    add(f_w_in[l], DFF + c4 * 512, KC, 128, 4, W1U[l], c4 * 4)
                if small:
                    for d4 in range(4):
                        add(f_w_out[l], d4 * 512, NFC, 128, 4, W2[l], d4 * 4)
                else:
                    for dc in range(KC):
                        add(f_w_out[l], dc * 128, NFC, 128, 1, W2[l], dc)
            elif kind == "A":
                j = idx
                for c4 in range(4):
                    add(a_w_in[j], c4 * 512, KC, 128, 4, AU[j], c4 * 4)
                    add(a_w_in[j], D + c4 * 512, KC, 256, 2, AV[j], c4 * 2)
                    add(a_w_out[j], c4 * 512, KC, 128, 4, AO[j], c4 * 4)
            else:
                j = idx
                for c4 in range(4):
                    add(b_w_qkv[j], c4 * 512, KC, 128, 4, BQ[j], c4 * 4)
                    add(b_w_out[j], c4 * 512, KC, 128, 4, BO[j], c4 * 4)
                add(b_w_qkv[j], 2048, KC, 128, 4, BK[j], 0)
                add(b_w_qkv[j], 2560, KC, 512, 1, BV[j], 0)
            return out

        def conv_load(step, sf_i, bf_i, dl_i):
            src2d, col0, a0, A, CWd, NB, dst4, ch0 = step
            n = A * NB * CWd
            src = src2d[a0 * 128:(a0 + A) * 128, col0:col0 + NB * CWd].rearrange("(a p) c -> p a c", p=128)
            f3 = sf_i[:, 0:n].rearrange("p (a c) -> p a c", a=A)
            P.add("sp", lambda e: e.dma_start(out=f3, in_=src), w=[bf_i], dsem=dl_i)

        def conv_cast_store(step, sf_i, sb_i, bf_i, bb_i, ds_i, ceng):
            src2d, col0, a0, A, CWd, NB, dst4, ch0 = step
            n = A * NB * CWd
            fin = sf_i[:, 0:n].rearrange("p (a n c) -> p n a c", a=A, n=NB)
            bout = sb_i[:, 0:n].rearrange("p (n a c) -> p n a c", n=NB, a=A)
            if ceng == "act":
                P.add("act", lambda e: e.copy(out=bout, in_=fin), r=[bf_i], w=[bb_i])
            else:
                P.add(ceng, lambda e: e.tensor_copy(out=bout, in_=fin), r=[bf_i], w=[bb_i])
            dd = dst4[ch0:ch0 + NB, :, a0 * CWd:(a0 + A) * CWd].rearrange("n p f -> p n f")
            bsrc = sb_i[:, 0:n].rearrange("p (n f) -> p n f", n=NB)
            P.add("pool", lambda e: e.dma_start(out=dd, in_=bsrc), r=[bb_i], dsem=ds_i)

        def prologue():
            AR.reset()
            zt = AR.f32(XW)
            zb_ = P.buf()
            zd = P.dsem()
            P.add("dve", lambda e: e.memset(zt, 0.0), w=[zb_])
            for xx in (xa, xb):
                for kc in range(KC):
                    P.add("pool", lambda e, xx=xx, kc=kc: e.dma_start(out=xx[kc * 128:(kc + 1) * 128, :], in_=zt),
                          r=[zb_], dsem=zd)
            bias_parts = []
            if any(k == "B" for k, _ in layers):
                rb_s = AR.f32(512)
                wm_s = AR.f32(384)
                bm_s = AR.f32(4096)
                acc = AR.f32(3 * 16 * 128)
                hi_ = AR.bf16(6144)
                lo_ = AR.bf16(6144)
                b1, b2 = P.buf(), P.buf()
                accb = [P.buf() for _ in range(48)]
                d1, d2, d3 = P.dsem(), P.dsem(), P.dsem()
                P.add("sp", lambda e: e.dma_start(out=rb_s, in_=rbr), w=[b1], dsem=d1)
                P.add("sp", lambda e: e.dma_start(out=wm_s, in_=wmask), w=[b1], dsem=d1)

                def bias_part(jj):
                    P.add("sp", lambda e: e.dma_start(out=bm_s, in_=bmask[jj]), w=[b2], dsem=d2)
                    for h in range(16):
                        o_ = acc[:, (jj * 16 + h) * 128:(jj * 16 + h + 1) * 128]
                        P.add("dve", lambda e, o_=o_: e.tensor_copy(
                            out=o_, in_=wm_s[:, jj * 128:(jj + 1) * 128]), r=[b1], w=[accb[jj * 16 + h]])
                    for b in range(32):
                        for h in range(16):
                            o_ = acc[:, (jj * 16 + h) * 128:(jj * 16 + h + 1) * 128]
                            P.add("dve", lambda e, o_=o_, b=b, h=h: e.scalar_tensor_tensor(
                                out=o_, in0=bm_s[:, b * 128:(b + 1) * 128],
                                scalar=rb_s[:, b * 16 + h:b * 16 + h + 1], in1=o_,
                                op0=ALU.mult, op1=ALU.add), r=[b1, b2], w=[accb[jj * 16 + h]])

                def bias_finish():
                    hlb = P.buf()
                    P.add("dve", lambda e: e.tensor_scalar(out=acc, in0=acc, scalar1=float(128.0 ** 0.5), scalar2=None,
                                                           op0=ALU.mult), r=accb, w=accb)
                    P.add("dve", lambda e: e.tensor_copy(out=hi_, in_=acc), r=accb, w=[hlb])
                    P.add("dve", lambda e: e.tensor_tensor(out=lo_, in0=acc, in1=hi_, op=ALU.subtract),
                          r=accb + [hlb], w=[hlb])
                    P.add("sp", lambda e: e.dma_start(out=biasd[:, 0:6144], in_=hi_), r=[hlb], dsem=d3)
                    P.add("sp", lambda e: e.dma_start(out=biasd[:, 6144:12288], in_=lo_), r=[hlb], dsem=d3)

                bias_parts = [lambda: bias_part(0), lambda: bias_part(1), lambda: bias_part(2), bias_finish]

            NST = 2
            sf = [AR.f32(8192) for _ in range(NST)]
            sb = [AR.bf16(8192) for _ in range(NST)]
            bf_ = [P.buf() for _ in range(NST)]
            bb_ = [P.buf() for _ in range(NST)]
            dl = [P.dsem() for _ in range(NST)]
            dst_ = [P.dsem() for _ in range(NST)]
            steps = steps_for(layers[0][0], layers[0][1], False) + steps_for("F", 0, False)
            every = max(1, len(steps) // 4)
            for k, stp in enumerate(steps):
                if k % every == 0 and bias_parts:
                    bias_parts.pop(0)()
                i = k % NST
                conv_load(stp, sf[i], bf_[i], dl[i])
                conv_cast_store(stp, sf[i], sb[i], bf_[i], bb_[i], dst_[i], "act")
            while bias_parts:
                bias_parts.pop(0)()
            P.barrier()

        prologue()

        def norm_a_steps(xsrc, t, st, jl=0, jh=T):
            c0 = t * T + 1
            xr, xrb, xrd, sq, sqb = st["xr"], st["xrb"], st["xrd"], st["sq"], st["sqb"]
            acc, accb = st["acc"], st["accb"]
            steps = []
            for kc in range(KC):
                box = {}

                def load(kc=kc, box=box):
                    i = st["n"] % len(xr)
                    st["n"] += 1
                    box["i"] = i
                    P.add("pool", lambda e: e.dma_start(
                        out=xr[i][:, jl:jh], in_=xsrc[kc * 128:(kc + 1) * 128, c0 + jl:c0 + jh]), w=[xrb[i]], dsem=xrd[i])

                def sqf(kc=kc, box=box):
                    i = box["i"]
                    k2 = kc % 2
                    if kc < 2:
                        P.add("act", lambda e: e.activation(out=acc[k2][:, jl:jh], in_=xr[i][:, jl:jh], func=AF.Square),
                              r=[xrb[i]], w=[accb[k2]])
                    else:
                        P.add("act", lambda e: e.activation(out=sq[k2][:, jl:jh], in_=xr[i][:, jl:jh], func=AF.Square),
                              r=[xrb[i]], w=[sqb[k2]])
                        P.add("dve", lambda e: e.tensor_tensor(out=acc[k2][:, jl:jh], in0=acc[k2][:, jl:jh],
                                                               in1=sq[k2][:, jl:jh], op=ALU.add),
                              r=[sqb[k2], accb[k2]], w=[accb[k2]])
                steps.append((load, sqf))
            return steps

        def norm_a(xsrc, t, st, jl=0, jh=T):
            for ld_, sq_ in norm_a_steps(xsrc, t, st, jl, jh):
                ld_()
                sq_()

        def norm_b_steps(xsrc, t, gi, hT, hT_b, st, msb=7, jl=0, jh=T):
            c0 = t * T + 1
            xr, xrb, xrd, rstd, rstdb = st["xr"], st["xrb"], st["xrd"], st["rstd"], st["rstdb"]
            acc, accb = st["acc"], st["accb"]
            ms = bank(msb)

            def head():
                for k2 in range(2):
                    P.add("pe", lambda e, k2=k2: e.matmul(ms[:, jl:jh], lhsT=ones_f, rhs=acc[k2][:, jl:jh],
                                                         start=(k2 == 0), stop=(k2 == 1)),
                          r=[accb[k2], cbuf], w=[pb[msb]])
                P.add("act", lambda e: e.activation(out=rstd[:, jl:jh], in_=ms[:, jl:jh], func=AF.Sqrt, scale=1.0 / D,
                                                    bias=eps_s[:, 0:1]), r=[pb[msb], cbuf], w=[rstdb])
                P.add("dve", lambda e: e.reciprocal(out=rstd[:, jl:jh], in_=rstd[:, jl:jh]), r=[rstdb], w=[rstdb])

            steps = []
            for kc in range(KC):
                box = {}

                def load(kc=kc, box=box):
                    i = st["n"] % len(xr)
                    st["n"] += 1
                    box["i"] = i
                    P.add("pool", lambda e: e.dma_start(
                        out=xr[i][:, jl:jh], in_=xsrc[kc * 128:(kc + 1) * 128, c0 + jl:c0 + jh]), w=[xrb[i]], dsem=xrd[i])

                def apply(kc=kc, box=box):
                    i = box["i"]
                    P.add("dve", lambda e: e.scalar_tensor_tensor(
                        out=hT[:, kc * T + jl:kc * T + jh], in0=xr[i][:, jl:jh],
                        scalar=gn_s[:, gi * KC + kc:gi * KC + kc + 1], in1=rstd[:, jl:jh],
                        op0=ALU.mult, op1=ALU.mult), r=[xrb[i], rstdb, cbuf], w=[hT_b[kc]])
                steps.append((load, apply))
            return head, steps

        def norm_b(xsrc, t, gi, hT, hT_b, st, msb=7, jl=0, jh=T):
            head, steps = norm_b_steps(xsrc, t, gi, hT, hT_b, st, msb, jl, jh)
            head()
            for ld_, ap_ in steps:
                ld_()
                ap_()

        class Spread:
            def __init__(self, hs, h0=0, lag=2):
                self.head, self.steps = hs if hs else (None, [])
                self.h0, self.lag, self.k = h0, lag, 0
                self.nl, self.na = 0, 0

            def tick(self):
                if self.head is None:
                    return
                k = self.k
                self.k += 1
                if k == self.h0:
                    self.head()
                if k > self.h0 and self.nl < KC:
                    self.steps[self.nl][0]()
                    self.nl += 1
                if k > self.h0 + self.lag and self.na < self.nl:
                    self.steps[self.na][1]()
                    self.na += 1

            def flush(self):
                if self.head is None:
                    return
                if self.k <= self.h0:
                    self.head()
                    self.k = self.h0 + 1
                while self.na < KC:
                    if self.nl < KC:
                        self.steps[self.nl][0]()
                        self.nl += 1
                    if self.na < self.nl and (self.nl - self.na > self.lag or self.nl == KC):
                        self.steps[self.na][1]()
                        self.na += 1

        def norm_state(NX=4):
            return dict(xr=[AR.f32(T) for _ in range(NX)], xrb=[P.buf() for _ in range(NX)],
                        xrd=[P.dsem() for _ in range(NX)],
                        sq=[AR.f32(T) for _ in range(2)], sqb=[P.buf() for _ in range(2)],
                        acc=[AR.f32(T) for _ in range(2)], accb=[P.buf() for _ in range(2)],
                        rstd=AR.f32(T), rstdb=P.buf(), n=0)

        def resid_state(NR=3):
            return dict(xr=[AR.f32(T) for _ in range(NR)], xb=[P.buf() for _ in range(NR)],
                        xd=[P.dsem() for _ in range(NR)], n=0)

        def resid_store(rs, xsrc, xdst, dc, t, obank, ml=0, mh=T):
            i = rs["n"] % len(rs["xr"])
            rs["n"] += 1
            xr_, xb_, xd_ = rs["xr"][i], rs["xb"][i], rs["xd"][i]
            c0 = rs["c0"](t)
            P.add("pool", lambda e: e.dma_start(out=xr_[:, ml:mh], in_=xsrc[dc * 128:(dc + 1) * 128, c0 + ml:c0 + mh]),
                  w=[xb_], dsem=xd_)
            P.add("dve", lambda e: e.tensor_tensor(out=xr_[:, ml:mh], in0=bank(obank)[:, ml:mh], in1=xr_[:, ml:mh], op=ALU.add),
                  r=[pb[obank], xb_], w=[xb_])
            P.add("pool", lambda e: e.dma_start(out=xdst[dc * 128:(dc + 1) * 128, c0 + ml:c0 + mh], in_=xr_[:, ml:mh]),
                  r=[xb_], dsem=xd_)

        def wring(n, words):
            return dict(t=[AR.bf16(words) for _ in range(n)], b=[P.buf() for _ in range(n)],
                        d=[P.dsem() for _ in range(n)], n=0)

        def wload(wr, src):
            i = wr["n"] % len(wr["t"])
            wr["n"] += 1
            tl, b, d = wr["t"][i], wr["b"][i], wr["d"][i]
            n = src.shape[-1]
            P.add("sp", lambda e: e.dma_start(out=tl[:, 0:n], in_=src), w=[b], dsem=d)
            return tl, b

        def ffn(l, xsrc, xdst, bg_steps=(), ranges=None):
            AR.reset()
            NBG = 3
            bgf = [AR.f32(2048) for _ in range(NBG)]
            bgb = [AR.bf16(2048) for _ in range(NBG)]
            bgfb = [P.buf() for _ in range(NBG)]
            bgbb = [P.buf() for _ in range(NBG)]
            bgdl = [P.dsem() for _ in range(NBG)]
            bgds = [P.dsem() for _ in range(NBG)]
            bgq = list(bg_steps)
            bgk = [0]
            pend = []

            def bg_tick():
                if pend:
                    stp, i = pend.pop()
                    conv_cast_store(stp, bgf[i], bgb[i], bgfb[i], bgbb[i], bgds[i], "act")
                if bgk[0] < len(bgq):
                    i = bgk[0] % NBG
                    stp = bgq[bgk[0]]
                    bgk[0] += 1
                    conv_load(stp, bgf[i], bgfb[i], bgdl[i])
                    pend.append((stp, i))

            hT2 = [AR.bf16(KC * T) for _ in range(2)]
            hT2_b = [[P.buf() for _ in range(KC)] for _ in range(2)]
            a = AR.bf16(NFC * T)
            a_b = [P.buf() for _ in range(NFC)]
            w1 = wring(3, KC * 128)
            w2 = wring(2, NFC * 128)
            st = norm_state(3)
            rs = resid_state()
            rs["c0"] = lambda t: t * T
            tg = [AR.f32(T) for _ in range(2)]
            tu = [AR.f32(T) for _ in range(2)]
            tgb = [P.buf() for _ in range(2)]
            tub = [P.buf() for _ in range(2)]
            zs = AR.f32(88 * 2)
            zsb = [P.buf() for _ in range(88)]
            wm = AR.f32(88 * 2)
            wmb = P.buf()
            P.add("dve", lambda e: e.memset(zs, 0.0), w=zsb)
            cwl = cw_s[:, l * 352:(l + 1) * 352]

            def cwc(ch, k):
                return cwl[:, ch * 4 + k:ch * 4 + k + 1]

            rg = ranges if ranges is not None else [("full", 0, T)] * NT

            def nrange(t):
                kind, jl, jh = rg[t]
                if kind == "zonly":
                    return max(0, jl - 126) // 128 * 128, jh
                return jl, jh

            norm_a(xsrc, 0, st, *nrange(0))
            norm_b(xsrc, 0, 4 + l, hT2[0], hT2_b[0], st, 7, *nrange(0))
            for t in range(NT):
                hT, hT_b = hT2[t % 2], hT2_b[t % 2]
                kind, jl, jh = rg[t]
                zonly = (kind == "zonly")
                ml, mh = (0 if jl == 0 else jl + 2), jh
                cw3 = cwl.rearrange("p (c k) -> p c k", k=4)
                wm3 = wm.rearrange("p (c k) -> p c k", k=2)
                P.add("dve", lambda e, t=t: e.tensor_scalar(out=wm3[:, :, 0:1], in0=cw3[:, :, 0:1],
                                                          scalar1=hm_s[:, t:t + 1], scalar2=None, op0=ALU.mult),
                      r=[cbuf], w=[wmb])
                P.add("dve", lambda e, t=t: e.tensor_scalar(out=wm3[:, :, 1:2], in0=cw3[:, :, 2:3],
                                                          scalar1=hm_s[:, t:t + 1], scalar2=None, op0=ALU.mult),
                      r=[cbuf], w=[wmb])
                nstf = norm_a_steps(xsrc, t + 1, st, *nrange(t + 1)) if t + 1 < NT else []
                for fc in range(NFC):
                    par = fc % 2
                    zb = {}
                    if fc % 2 == 0:
                        bg_tick()
                    if nstf and 20 <= fc < 20 + KC:
                        nstf[fc - 20][0]()
                    if nstf and 22 <= fc < 22 + KC:
                        nstf[fc - 22][1]()
                    for gu, Wd in ((0, W1G), (1, W1U)):
                        wt, wb_ = wload(w1, Wd[l, fc])
                        bk = par * 2 + gu
                        zb[gu] = bk
                        for kc in range(KC):
                            P.add("pe", lambda e, wt=wt, kc=kc, bk=bk, hT=hT, jl=jl, jh=jh: e.matmul(
                                bank(bk)[:, jl:jh], lhsT=wt[:, kc * 128:(kc + 1) * 128],
                                rhs=hT[:, kc * T + jl:kc * T + jh],
                                start=(kc == 0), stop=(kc == KC - 1)), r=[wb_, hT_b[kc]], w=[pb[bk]])
                    for gu in (0, 1):
                        ch = gu * NFC + fc
                        Pz = bank(zb[gu])
                        pbk = pb[zb[gu]]
                        tt = (tg if gu == 0 else tu)[par]
                        ttb = (tgb if gu == 0 else tub)[par]
                        s0 = zs[:, ch * 2:ch * 2 + 1]
                        s1 = zs[:, ch * 2 + 1:ch * 2 + 2]
                        w0, w1_, w2_, bb = cwc(ch, 0), cwc(ch, 1), cwc(ch, 2), cwc(ch, 3)
                        w0m = wm[:, ch * 2:ch * 2 + 1]
                        w2m = wm[:, ch * 2 + 1:ch * 2 + 2]
                        if not zonly:
                            P.add("act", lambda e, tt=tt, Pz=Pz, w1_=w1_, bb=bb, jl=jl, jh=jh: e.activation(
                                out=tt[:, jl + 1:jh], in_=Pz[:, jl:jh - 1], func=AF.Identity, scale=w1_, bias=bb),
                                r=[pbk, cbuf], w=[ttb])
                            if jl == 0:
                                P.add("act", lambda e, tt=tt, s1=s1, w1_=w1_, bb=bb: e.activation(
                                    out=tt[:, 0:1], in_=s1, func=AF.Identity, scale=w1_, bias=bb),
                                    r=[zsb[ch], cbuf], w=[ttb])
                            P.add("dve", lambda e, tt=tt, Pz=Pz, w0=w0, jl=jl, jh=jh: e.scalar_tensor_tensor(
                                out=tt[:, jl + 2:jh], in0=Pz[:, jl:jh - 2], scalar=w0, in1=tt[:, jl + 2:jh],
                                op0=ALU.mult, op1=ALU.add), r=[pbk, cbuf], w=[ttb])
                            P.add("dve", lambda e, tt=tt, Pz=Pz, w2_=w2_, jl=jl, jh=jh: e.scalar_tensor_tensor(
                                out=tt[:, jl + 1:jh], in0=Pz[:, jl + 1:jh], scalar=w2_, in1=tt[:, jl + 1:jh],
                                op0=ALU.mult, op1=ALU.add), r=[pbk, cbuf], w=[ttb])
                            if jl == 0:
                                P.add("dve", lambda e, tt=tt, s0=s0, w0=w0: e.scalar_tensor_tensor(
                                    out=tt[:, 0:1], in0=s0, scalar=w0, in1=tt[:, 0:1],
                                    op0=ALU.mult, op1=ALU.add), r=[zsb[ch], cbuf], w=[ttb])
                                P.add("dve", lambda e, tt=tt, Pz=Pz, w2m=w2m: e.scalar_tensor_tensor(
                                    out=tt[:, 0:1], in0=Pz[:, 0:1], scalar=w2m, in1=tt[:, 0:1],
                                    op0=ALU.mult, op1=ALU.add), r=[pbk, wmb], w=[ttb])
                                P.add("dve", lambda e, tt=tt, s1=s1, w0m=w0m: e.scalar_tensor_tensor(
                                    out=tt[:, 1:2], in0=s1, scalar=w0m, in1=tt[:, 1:2],
                                    op0=ALU.mult, op1=ALU.add), r=[zsb[ch], wmb], w=[ttb])
                        if jh == T:
                            P.add("dve", lambda e, Pz=Pz, ch=ch: e.tensor_copy(
                                out=zs[:, ch * 2:ch * 2 + 2], in_=Pz[:, T - 2:T]), r=[pbk], w=[zsb[ch]])
                    if zonly:
                        continue
                    P.add("act", lambda e, par=par, ml=ml, mh=mh: e.activation(
                        out=tg[par][:, ml:mh], in_=tg[par][:, ml:mh], func=AF.Silu),
                        r=[tgb[par]], w=[tgb[par]])
                    P.add("dve", lambda e, par=par, fc=fc, ml=ml, mh=mh: e.tensor_tensor(
                        out=a[:, fc * T + ml:fc * T + mh], in0=tg[par][:, ml:mh], in1=tu[par][:, ml:mh], op=ALU.mult),
                        r=[tgb[par], tub[par]], w=[a_b[fc]])
                spr = Spread(norm_b_steps(xsrc, t + 1, 4 + l, hT2[(t + 1) % 2], hT2_b[(t + 1) % 2], st, 7,
                                          *nrange(t + 1)) if t + 1 < NT else None)
                for dc in range(KC):
                    spr.tick()
                    if dc % 2 == 0:
                        bg_tick()
                    if zonly:
                        continue
                    wt, wb_ = wload(w2, W2[l, dc])
                    ob = 4 + dc % 2
                    for fc in range(NFC):
                        P.add("pe", lambda e, wt=wt, fc=fc, ob=ob, ml=ml, mh=mh: e.matmul(
                            bank(ob)[:, ml:mh], lhsT=wt[:, fc * 128:(fc + 1) * 128], rhs=a[:, fc * T + ml:fc * T + mh],
                            start=(fc == 0), stop=(fc == NFC - 1)), r=[wb_, a_b[fc]], w=[pb[ob]])
                    resid_store(rs, xsrc, xdst, dc, t, ob, ml, mh)
                spr.flush()
            while pend or bgk[0] < len(bgq):
                bg_tick()
            P.barrier()

        def mixer_a(j, gi, xsrc, xdst, blocks=None):
            AR.reset()
            hT2 = [AR.bf16(KC * T) for _ in range(2)]
            hT2_b = [[P.buf() for _ in range(KC)] for _ in range(2)]
            vn = AR.bf16(4 * D)
            vn_b = [P.buf() for _ in range(4)]
            yT = AR.bf16(KC * T)
            yT_b = [P.buf() for _ in range(KC)]
            wu = wring(3, KC * 128)
            wv = wring(3, KC * 256)
            st = norm_state()
            rs = resid_state()
            rs["c0"] = lambda t: t * T + 1
            bvb = AR.f32(D)
            vnb = AR.f32(D)
            bsb = AR.f32(1024)
            wsf = AR.f32(1024)
            wsb = AR.bf16(1024)
            lb = P.buf()
            ld = P.dsem()
            P.add("sp", lambda e: e.dma_start(out=bvb, in_=abv[j]), w=[lb], dsem=ld)
            P.add("sp", lambda e: e.dma_start(out=vnb, in_=avn[j]), w=[lb], dsem=ld)
            P.add("sp", lambda e: e.dma_start(out=bsb, in_=absr[j]), w=[lb], dsem=ld)
            P.add("sp", lambda e: e.dma_start(out=wsf, in_=wst[j]), w=[lb], dsem=ld)
            P.add("dve", lambda e: e.tensor_copy(out=wsb, in_=wsf), r=[lb], w=[lb])
            vg4 = [yT.bitcast(F32)[:, 0:D], yT.bitcast(F32)[:, D:2 * D], AR.f32(D), AR.f32(D)]
            vg4b = [[P.buf()] + yT_b[0:8], [P.buf()] + yT_b[8:16], [P.buf()], [P.buf()]]
            ssq = AR.f32(16)
            ssb = P.buf()
            uc_ = [AR.f32(T) for _ in range(2)]
            ucb = [P.buf() for _ in range(2)]
            tmp = [AR.f32(T) for _ in range(2)]
            tmb = [P.buf() for _ in range(2)]
            junk = AR.bf16(D)
            jb = P.buf()
            norm_a(xsrc, 0, st)
            norm_b(xsrc, 0, gi, hT2[0], hT2_b[0], st)
            for t in range(NT):
                hT, hT_b = hT2[t % 2], hT2_b[t % 2]
                bl = (blocks or {}).get(t, [0, 1, 2, 3])
                cl, ch_ = bl[0] * 128, (bl[-1] + 1) * 128
                for cg in range(8):
                    wt, wb_ = wload(wv, AV[j, cg])
                    for b in bl:
                        bk = (cg * 4 + b) % 2
                        for kc in range(KC):
                            P.add("pe", lambda e, wt=wt, kc=kc, bk=bk, b=b, hT=hT: e.matmul(
                                bank(bk)[:, 0:256], lhsT=hT[:, kc * T + b * 128:kc * T + (b + 1) * 128],
                                rhs=wt[:, kc * 256:(kc + 1) * 256],
                                start=(kc == 0), stop=(kc == KC - 1)), r=[wb_, hT_b[kc]], w=[pb[bk]])
                        P.add("dve", lambda e, bk=bk, cg=cg, b=b: e.tensor_tensor(
                            out=vg4[b][:, cg * 256:(cg + 1) * 256], in0=bank(bk)[:, 0:256],
                            in1=bvb[:, cg * 256:(cg + 1) * 256], op=ALU.add),
                            r=[pb[bk], lb], w=vg4b[b])
                nsteps = norm_a_steps(xsrc, t + 1, st) if t + 1 < NT else []
                for b in bl:
                    P.add("act", lambda e, b=b: e.activation(out=vg4[b], in_=vg4[b], func=AF.Gelu_apprx_tanh),
                          r=vg4b[b], w=vg4b[b])
                    P.add("act", lambda e, b=b: e.activation(out=junk, in_=vg4[b], func=AF.Square,
                                                             accum_out=ssq[:, b:b + 1]),
                          r=vg4b[b], w=[jb, ssb])
                P.add("act", lambda e: e.activation(out=ssq[:, 4:8], in_=ssq[:, 0:4], func=AF.Sqrt,
                                                    scale=1.0 / D, bias=eps_s[:, 0:1]), r=[ssb, cbuf], w=[ssb])
                P.add("dve", lambda e: e.reciprocal(out=ssq[:, 8:12], in_=ssq[:, 4:8]), r=[ssb], w=[ssb])
                for b in bl:
                    P.add("dve", lambda e, b=b: e.scalar_tensor_tensor(
                        out=vn[:, b * D:(b + 1) * D], in0=vg4[b], scalar=ssq[:, 8 + b:9 + b], in1=vnb,
                        op0=ALU.mult, op1=ALU.mult), r=vg4b[b] + [ssb, lb], w=[vn_b[b]])
                for cc in range(KC):
                    wt, wb_ = wload(wu, AU[j, cc])
                    ub = 2 + cc % 2
                    sbk = 4 + cc % 2
                    p2 = cc % 2
                    if nsteps:
                        nsteps[cc][0]()
                        if cc >= 1:
                            nsteps[cc - 1][1]()
                    for kc in range(KC):
                        P.add("pe", lambda e, wt=wt, kc=kc, ub=ub, hT=hT, cl=cl, ch_=ch_: e.matmul(
                            bank(ub)[:, cl:ch_], lhsT=wt[:, kc * 128:(kc + 1) * 128], rhs=hT[:, kc * T + cl:kc * T + ch_],
                            start=(kc == 0), stop=(kc == KC - 1)), r=[wb_, hT_b[kc]], w=[pb[ub]])
                    g = cc // 2
                    for b in bl:
                        P.add("pe", lambda e, b=b, cc=cc, g=g, sbk=sbk: e.matmul(
                            bank(sbk)[:, b * 128:(b + 1) * 128],
                            lhsT=vn[:, b * D + cc * 128:b * D + (cc + 1) * 128],
                            rhs=wsb[:, g * 128:(g + 1) * 128], start=True, stop=True),
                            r=[vn_b[b], lb], w=[pb[sbk]])
                    P.add("act", lambda e, ub=ub, p2=p2, cc=cc, cl=cl, ch_=ch_: e.activation(
                        out=uc_[p2][:, cl:ch_], in_=bank(ub)[:, cl:ch_], func=AF.Gelu_apprx_tanh,
                        bias=abu_s[:, j * KC + cc:j * KC + cc + 1]), r=[pb[ub], cbuf], w=[ucb[p2]])
                    for b in bl:
                        P.add("dve", lambda e, b=b, g=g, sbk=sbk, p2=p2: e.tensor_tensor(
                            out=tmp[p2][:, b * 128:(b + 1) * 128], in0=bank(sbk)[:, b * 128:(b + 1) * 128],
                            in1=bsb[:, g * 128:(g + 1) * 128], op=ALU.add), r=[pb[sbk], lb], w=[tmb[p2]])
                    P.add("dve", lambda e, p2=p2, cc=cc, cl=cl, ch_=ch_: e.tensor_tensor(
                        out=yT[:, cc * T + cl:cc * T + ch_], in0=tmp[p2][:, cl:ch_], in1=uc_[p2][:, cl:ch_], op=ALU.mult),
                        r=[tmb[p2], ucb[p2]], w=[yT_b[cc]])
                if nsteps:
                    nsteps[KC - 1][1]()
                spr = Spread(norm_b_steps(xsrc, t + 1, gi, hT2[(t + 1) % 2], hT2_b[(t + 1) % 2], st)
                             if t + 1 < NT else None, h0=1)
                for dc in range(KC):
                    spr.tick()
                    wt, wb_ = wload(wu, AO[j, dc])
                    ob = 6 if dc % 2 == 0 else 0
                    for kc in range(KC):
                        P.add("pe", lambda e, wt=wt, kc=kc, ob=ob, cl=cl, ch_=ch_: e.matmul(
                            bank(ob)[:, cl:ch_], lhsT=wt[:, kc * 128:(kc + 1) * 128], rhs=yT[:, kc * T + cl:kc * T + ch_],
                            start=(kc == 0), stop=(kc == KC - 1)), r=[wb_, yT_b[kc]], w=[pb[ob]])
                    resid_store(rs, xsrc, xdst, dc, t, ob, cl, ch_)
                spr.flush()
            P.barrier()

        def mixer_b(j, gi, xsrc, xdst, qblocks=None):
            AR.reset()
            RT = 3
            RB = RT * 4
            hT2 = [AR.bf16(KC * T) for _ in range(2)]
            hT2_b = [[P.buf() for _ in range(KC)] for _ in range(2)]
            KT = AR.bf16(4 * RB * 128)
            KT_b = [[P.buf() for _ in range(4)] for _ in range(RT)]
            Vt = AR.bf16(RB * 512)
            V_b = [P.buf() for _ in range(RB)]
            qT = AR.bf16(KC * T)
            qT_b = [P.buf() for _ in range(KC)]
            oall = AR.bf16(KC * T)
            oall_b = [P.buf() for _ in range(KC)]
            wq = wring(3, KC * 128)
            wvr = wring(1, KC * 512)
            st = norm_state(3)
            rs = resid_state(2)
            rs["c0"] = lambda t: t * T + 1
            bT = AR.bf16(2 * 3 * 16 * 128)
            idf = AR.f32(128)
            idb = AR.bf16(128)
            es_ = AR.f32(16)
            lb = P.buf()
            ld = P.dsem()
            P.add("sp", lambda e: e.dma_start(out=bT, in_=biasd), w=[lb], dsem=ld)
            P.add("sp", lambda e: e.dma_start(out=es_, in_=sink[j]), w=[lb], dsem=ld)
            P.add("act", lambda e: e.activation(out=es_, in_=es_, func=AF.Exp), r=[lb], w=[lb])
            P.add("sp", lambda e: e.dma_start(out=idf, in_=identd), w=[lb], dsem=ld)
            P.add("dve", lambda e: e.tensor_copy(out=idb, in_=idf), r=[lb], w=[lb])
            NE = 8
            pT = [AR.bf16(T) for _ in range(NE)]
            pTb = [P.buf() for _ in range(NE)]
            ds_ = [AR.f32(T) for _ in range(2)]
            dsb = [P.buf() for _ in range(2)]
            vt, vtb = wload(wvr, BV[j, 0])
            scale = 128.0 ** -0.5
            cnt = {"u": 0, "g": 0}

            def rp(kb):
                return ((kb // 4) % RT) * 4 + kb % 4

            def phase1(t):
                hp = t % 2
                slot = t % RT
                for kh in range(4):
                    wt, wb_ = wload(wq, BK[j, kh])
                    bk = kh % 2
                    for kc in range(KC):
                        P.add("pe", lambda e, wt=wt, kc=kc, bk=bk, hp=hp: e.matmul(
                            bank(bk), lhsT=wt[:, kc * 128:(kc + 1) * 128], rhs=hT2[hp][:, kc * T:(kc + 1) * T],
                            start=(kc == 0), stop=(kc == KC - 1)), r=[wb_, hT2_b[hp][kc]], w=[pb[bk]])
                    o_ = KT[:, kh * RB * 128 + slot * 512:kh * RB * 128 + (slot + 1) * 512]
                    P.add("act", lambda e, o_=o_, bk=bk: e.copy(out=o_, in_=bank(bk)), r=[pb[bk]], w=[KT_b[slot][kh]])
                for b in range(4):
                    bk = 2 + b % 2
                    gb = rp(t * 4 + b)
                    for kc in range(KC):
                        P.add("pe", lambda e, kc=kc, bk=bk, b=b, hp=hp: e.matmul(
                            bank(bk), lhsT=hT2[hp][:, kc * T + b * 128:kc * T + (b + 1) * 128],
                            rhs=vt[:, kc * 512:(kc + 1) * 512], start=(kc == 0), stop=(kc == KC - 1)),
                            r=[vtb, hT2_b[hp][kc]], w=[pb[bk]])
                    P.add("dve", lambda e, gb=gb, bk=bk: e.tensor_copy(out=Vt[:, gb * 512:(gb + 1) * 512], in_=bank(bk)),
                          r=[pb[bk]], w=[V_b[gb]])

            def logits(u):
                b, kh, i, jj, n_, nj, g = u["b"], u["kh"], u["i"], u["jj"], u["n"], u["nj"], u["g"]
                s_ = cnt["u"]
                cnt["u"] += 1
                u["s"] = s_
                lbk, ei = s_ % 4, s_ % NE
                kb = i - 1 + jj
                kslot = (kb // 4) % RT
                kcol = kh * RB * 128 + rp(kb) * 128
                q3 = qT.rearrange("p (h n) -> p h n", h=KC)[:, kh * 4:(kh + 1) * 4, b * 128:(b + 1) * 128]
                qr = [qT_b[kh * 4 + hh] for hh in range(4)]
                o3_ = bank(lbk).rearrange("p (h n) -> p h n", h=4)
                P.add("pe", lambda e: e.matmul(o3_, lhsT=KT[:, kcol:kcol + 128], rhs=q3,
                                               start=True, stop=False), r=[KT_b[kslot][kh]] + qr, w=[pb[lbk]])
                for hl in range(2):
                    b3 = bT.rearrange("p (s j h n) -> p s j h n", s=2, j=3, h=16)[:, hl, jj, kh * 4:(kh + 1) * 4, :]
                    P.add("pe", lambda e, b3=b3, hl=hl: e.matmul(o3_, lhsT=idb, rhs=b3, start=False, stop=(hl == 1)),
                          r=[lb], w=[pb[lbk]])
                P.add("act", lambda e: e.activation(
                    out=pT[ei], in_=bank(lbk), func=AF.Exp, scale=scale, bias=mv_s[:, i * 3 + jj:i * 3 + jj + 1]),
                    r=[pb[lbk], cbuf], w=[pTb[ei]])

            def pvden(u, t):
                b, kh, i, jj, n_, nj, g = u["b"], u["kh"], u["i"], u["jj"], u["n"], u["nj"], u["g"]
                ei = u["s"] % NE
                ob = 4 + 2 * (g % 2)
                db = ob + 1
                di = g % 2
                vb_ = rp(i - 1 + jj)
                P.add("pe", lambda e: e.matmul(
                    bank(ob), lhsT=Vt[:, vb_ * 512 + kh * 128:vb_ * 512 + (kh + 1) * 128], rhs=pT[ei],
                    start=(n_ == 0), stop=(n_ == nj - 1)), r=[V_b[vb_], pTb[ei]], w=[pb[ob]])
                P.add("pe", lambda e: e.matmul(
                    bank(db), lhsT=ones_b, rhs=pT[ei],
                    start=(n_ == 0), stop=(n_ == nj - 1)), r=[cbuf, pTb[ei]], w=[pb[db]])
                if n_ != nj - 1:
                    return
                for hh in range(4):
                    h = kh * 4 + hh
                    P.add("act", lambda e, hh=hh, h=h: e.activation(
                        out=ds_[di][:, hh * 128:(hh + 1) * 128], in_=bank(db)[:, hh * 128:(hh + 1) * 128],
                        func=AF.Ln, bias=es_[:, h:h + 1]), r=[pb[db], lb], w=[dsb[di]])
                P.add("act", lambda e: e.activation(out=ds_[di], in_=ds_[di], func=AF.Exp, scale=-1.0),
                      r=[dsb[di]], w=[dsb[di]])
                o3 = oall.rearrange("p (h n) -> p h n", h=KC)[:, kh * 4:(kh + 1) * 4, b * 128:(b + 1) * 128]
                P.add("dve", lambda e: e.tensor_tensor(
                    out=o3, in0=bank(ob).rearrange("p (h n) -> p h n", h=4),
                    in1=ds_[di].rearrange("p (h n) -> p h n", h=4), op=ALU.mult),
                    r=[pb[ob], dsb[di]], w=[oall_b[kh * 4 + hh] for hh in range(4)])

            def phase2(t):
                hp = t % 2
                bl = (qblocks or {}).get(t, [0, 1, 2, 3])
                cl, ch_ = bl[0] * 128, (bl[-1] + 1) * 128
                for hq in range(KC):
                    wt, wb_ = wload(wq, BQ[j, hq])
                    bk = hq % 2
                    for kc in range(KC):
                        P.add("pe", lambda e, wt=wt, kc=kc, bk=bk, hp=hp: e.matmul(
                            bank(bk)[:, cl:ch_], lhsT=wt[:, kc * 128:(kc + 1) * 128],
                            rhs=hT2[hp][:, kc * T + cl:kc * T + ch_],
                            start=(kc == 0), stop=(kc == KC - 1)), r=[wb_, hT2_b[hp][kc]], w=[pb[bk]])
                    P.add("act", lambda e, hq=hq, bk=bk: e.copy(out=qT[:, hq * T + cl:hq * T + ch_], in_=bank(bk)[:, cl:ch_]),
                          r=[pb[bk]], w=[qT_b[hq]])
                nsteps = norm_a_steps(xsrc, t + 2, st) if t + 2 < NT else []
                units = []
                for b in bl:
                    i = t * 4 + b
                    for kh in range(4):
                        js = [jj for jj in range(3) if 0 <= i - 1 + jj < NBLK]
                        g = cnt["g"]
                        cnt["g"] += 1
                        for n_, jj in enumerate(js):
                            units.append(dict(b=b, kh=kh, i=i, jj=jj, n=n_, nj=len(js), g=g))
                SK = 4
                spr = Spread(norm_b_steps(xsrc, t + 2, gi, hT2[hp], hT2_b[hp], st, msb=3)
                             if t + 2 < NT else None, h0=KC + 2)
                for k in range(max(len(units) + SK, KC + 2 if nsteps else 0)):
                    if nsteps and k < KC:
                        nsteps[k][0]()
                    if nsteps and 2 <= k < KC + 2:
                        nsteps[k - 2][1]()
                    spr.tick()
                    if k < len(units):
                        logits(units[k])
                    if 0 <= k - SK < len(units):
                        pvden(units[k - SK], t)
                spr.flush()
                for dc in range(KC):
                    wt, wb_ = wload(wq, BO[j, dc])
                    ob = 2 + dc % 2
                    for kc in range(KC):
                        P.add("pe", lambda e, wt=wt, kc=kc, ob=ob: e.matmul(
                            bank(ob)[:, cl:ch_], lhsT=wt[:, kc * 128:(kc + 1) * 128], rhs=oall[:, kc * T + cl:kc * T + ch_],
                            start=(kc == 0), stop=(kc == KC - 1)), r=[wb_, oall_b[kc]], w=[pb[ob]])
                    resid_store(rs, xsrc, xdst, dc, t, ob, cl, ch_)

            norm_a(xsrc, 0, st)
            norm_b(xsrc, 0, gi, hT2[0], hT2_b[0], st)
            phase1(0)
            if NT > 1:
                norm_a(xsrc, 1, st)
                norm_b(xsrc, 1, gi, hT2[1], hT2_b[1], st)
            for t in range(NT):
                if t + 1 < NT:
                    phase1(t + 1)
                phase2(t)
            P.barrier()

        def final_norm(xsrc):
            AR.reset()
            xt = [AR.f32(KC * T) for _ in range(2)]
            xtb = [P.buf() for _ in range(2)]
            xtd = [P.dsem() for _ in range(2)]
            sq = [AR.f32(T) for _ in range(2)]
            sqb = [P.buf() for _ in range(2)]
            acc = AR.f32(T)
            accb = P.buf()
            rstd = AR.f32(T)
            rstdb = P.buf()
            n = 0
            for t in range(NT):
                lo = max(t * T, out_off)
                hi = min((t + 1) * T, out_off + n_out_tok)
                if lo >= hi:
                    continue
                c0 = t * T + 1
                i = n % 2
                n += 1
                x3 = xt[i].rearrange("p (k n) -> p k n", k=KC)
                src = xsrc[:, c0:c0 + T].rearrange("(k p) n -> p k n", p=128)
                P.add("sp", lambda e, x3=x3, src=src: e.dma_start(out=x3, in_=src), w=[xtb[i]], dsem=xtd[i])
                for kc in range(KC):
                    k2 = kc % 2
                    xk = xt[i][:, kc * T:(kc + 1) * T]
                    if kc == 0:
                        P.add("act", lambda e, xk=xk: e.activation(out=acc, in_=xk, func=AF.Square), r=[xtb[i]], w=[accb])
                    else:
                        P.add("act", lambda e, xk=xk, k2=k2: e.activation(out=sq[k2], in_=xk, func=AF.Square),
                              r=[xtb[i]], w=[sqb[k2]])
                        P.add("dve", lambda e, k2=k2: e.tensor_tensor(out=acc, in0=acc, in1=sq[k2], op=ALU.add),
                              r=[sqb[k2], accb], w=[accb])
                P.add("pe", lambda e: e.matmul(bank(7), lhsT=ones_f, rhs=acc, start=True, stop=True),
                      r=[accb, cbuf], w=[pb[7]])
                P.add("act", lambda e: e.activation(out=rstd, in_=bank(7), func=AF.Sqrt, scale=1.0 / D,
                                                    bias=eps_s[:, 0:1]), r=[pb[7], cbuf], w=[rstdb])
                P.add("dve", lambda e: e.reciprocal(out=rstd, in_=rstd), r=[rstdb], w=[rstdb])
                for kc in range(KC):
                    xk = xt[i][:, kc * T:(kc + 1) * T]
                    P.add("dve", lambda e, xk=xk, kc=kc: e.scalar_tensor_tensor(
                        out=xk, in0=xk, scalar=gn_s[:, 8 * KC + kc:8 * KC + kc + 1], in1=rstd,
                        op0=ALU.mult, op1=ALU.mult), r=[xtb[i], rstdb, cbuf], w=[xtb[i]])
                dst = y_out[:, lo - out_off:hi - out_off].rearrange("(k p) n -> p k n", p=128)
                P.add("sp", lambda e, x3=x3, dst=dst, lo=lo, hi=hi, t=t: e.dma_start(
                    out=dst, in_=x3[:, :, lo - t * T:hi - t * T]), r=[xtb[i]], dsem=xtd[i])
            P.barrier()

        cur, nxt = x_in, xa
        for l, (kind, j) in enumerate(layers):
            msub = None
            if isinstance(trim, dict):
                msub = trim.get(("M", l))
            elif trim and NT == 8 and len(layers) == 4:
                msub = {1: {0: [1, 2, 3], 7: [0, 1, 2]}, 2: {0: [2, 3], 7: [0, 1]}, 3: {0: [3], 7: [0]}}.get(l)
            if kind == "A":
                mixer_a(j, l, cur, nxt, msub)
            else:
                mixer_b(j, l, cur, nxt, msub)
            cur, nxt = nxt, (xb if nxt is xa else xa)
            bg = []
            if l + 1 < len(layers):
                bg = steps_for(layers[l + 1][0], layers[l + 1][1], True) + steps_for("F", l + 1, True)
            rgs = None
            if isinstance(trim, dict):
                rgs = trim.get(l)
            elif trim and NT == 8 and len(layers) == 4:
                lo_hi = {0: (124, 388), 1: (252, 260), 2: (380, 132)}
                if l < 3:
                    jl0, jh7 = lo_hi[l]
                    rgs = [("full", jl0, T)] + [("full", 0, T)] * 6 + [("full", 0, jh7)]
                else:
                    rgs = [("zonly", 510, T)] + [("full", 0, T)] * 6 + [("full", 0, 4)]
            ffn(l, cur, nxt, bg, rgs)
            cur, nxt = nxt, (xb if nxt is xa else xa)
        final_norm(cur)

        P.finalize()
        with nc.Block() as block:
            @block.tensor
            def _(e):
                P.emit("pe", e)

            @block.scalar
            def _(e):
                P.emit("act", e)

            @block.vector
            def _(e):
                P.emit("dve", e)

            @block.gpsimd
            def _(e):
                P.emit("pool", e)

            @block.sync
            def _(e):
                P.emit("sp", e)
    return nc


def _relative_bucket(rel):
    half, max_exact = 16, 8
    ret = (rel > 0).astype(np.int32) * half
    n = np.abs(rel)
    nf = np.maximum(n, 1).astype(np.float32)
    large = max_exact + (np.log(nf / max_exact) / np.log(128 / max_exact) * (half - max_exact)).astype(np.int32)
    large = np.minimum(large, half - 1)
    return (ret + np.where(n < max_exact, n, large)).astype(np.int32)


def static_tables():
    c = np.arange(128)[:, None]
    q = np.arange(128)[None, :]
    bm = np.zeros((3, 128, 32, 128), np.float32)
    wm = np.zeros((128, 3, 128), np.float32)
    for jj in range(3):
        rel = (jj - 1) * 128 + c - q
        bk = _relative_bucket(rel)
        for b in range(32):
            bm[jj, :, b, :] = (bk == b)
        wm[:, jj, :] = np.where(np.abs(rel) <= 128, 0.0, NEG)
    return bm.reshape(3, 128, 32 * 128), wm.reshape(128, 3 * 128)


def per_partition(v):
    v = np.asarray(v, np.float32)
    lead = v.shape[:-1]
    n = v.shape[-1] // 128
    return np.ascontiguousarray(np.moveaxis(v.reshape(*lead, n, 128), -1, 0))


def shared_inputs(inp):
    f = np.float32
    d = {}
    gn = np.concatenate([inp["mix_norm"], inp["ffn_norm"], inp["final_norm"][None]], 0)
    d["gn"] = per_partition(gn).reshape(128, 9 * KC)
    cw = np.concatenate([inp["f_conv_w"], inp["f_conv_b"][:, None, :]], 1)
    cwp = per_partition(cw)
    d["cw"] = np.ascontiguousarray(cwp.transpose(0, 1, 3, 2)).reshape(128, 4 * 88 * 4)
    d["abu"] = per_partition(inp["a_b_in"][:, :D]).reshape(128, 2 * KC)
    d["abv"] = np.ascontiguousarray(np.broadcast_to(inp["a_b_in"][:, None, D:], (2, 128, D))).astype(f)
    d["avn"] = np.ascontiguousarray(np.broadcast_to(inp["a_v_norm"][:, None, :], (2, 128, D))).astype(f)
    d["absr"] = np.ascontiguousarray(np.broadcast_to(inp["a_b_s"].reshape(2, 1, 1024), (2, 128, 1024))).astype(f)
    d["wst"] = np.ascontiguousarray(np.asarray(inp["a_w_s"], f).transpose(0, 3, 1, 2)).reshape(2, 128, 1024)
    d["sink"] = np.ascontiguousarray(np.broadcast_to(np.asarray(inp["b_sink"], f)[:, None, :], (2, 128, 16)))
    d["rbr"] = np.ascontiguousarray(np.broadcast_to(np.asarray(inp["rel_bias"], f).reshape(1, 512), (128, 512)))
    bm, wm = static_tables()
    d["bmask"], d["wmask"] = bm, wm
    d["ident"] = np.eye(128, dtype=np.float32)
    for k in ("a_w_in", "a_w_out", "b_w_qkv", "b_w_out", "f_w_in", "f_w_out"):
        d[k] = np.ascontiguousarray(np.asarray(inp[k], f))
    return d


def window_inputs(xflat_T, start, NT, bounds, ntok):
    W = NT * T
    xw = np.zeros((D, W + 2), np.float32)
    lo, hi = max(start, 0), min(start + W, ntok)
    if hi > lo:
        xw[:, 1 + lo - start:1 + hi - start] = xflat_T[:, lo:hi]
    bset = set(bounds)
    hm = np.ones((NT,), np.float32)
    for t in range(NT):
        if t == 0 or (start + t * T) in bset:
            hm[t] = 0.0
    NBLK = W // 128
    mv = np.zeros((NBLK, 3), np.float32)
    for i in range(NBLK):
        s = start + i * 128
        if s in bset or i == 0:
            mv[i, 0] = NEG
        if (s + 128) in bset or i == NBLK - 1:
            mv[i, 2] = NEG
    return {"x_in": xw,
            "hmask": np.ascontiguousarray(np.broadcast_to(hm[None], (128, NT))),
            "maskv": np.ascontiguousarray(np.broadcast_to(mv.reshape(1, -1), (128, NBLK * 3)))}


_CACHE = {}


def kernel(**inp):
    NT = (OWN + 2 * HALO) // T
    layers = (("A", 0), ("B", 0), ("A", 1), ("B", 1))
    key = "full"
    if key not in _CACHE:
        _CACHE[key] = build(NT, layers, OWN, HALO)
    nc = _CACHE[key]
    xp = np.asarray(inp["x_prompt"], np.float32).reshape(-1, D)
    xs = np.asarray(inp["x_sample"], np.float32).reshape(-1, D)
    xflat_T = np.ascontiguousarray(np.concatenate([xp, xs], 0).T)
    sh = shared_inputs(inp)
    in_maps = []
    for c in range(8):
        m = dict(sh)
        m.update(window_inputs(xflat_T, c * OWN - HALO, NT, SEQ_BOUNDS, NTOK))
        in_maps.append(m)
    res = run_bass_kernel_spmd(nc, in_maps, core_ids=list(range(8)))
    yT = np.concatenate([np.asarray(r["y_out"]) for r in res.results], axis=1)
    y = np.ascontiguousarray(yT.T)
    n_p = xp.shape[0]
    return (y[:n_p].reshape(inp["x_prompt"].shape).astype(np.float32),
            y[n_p:].reshape(inp["x_sample"].shape).astype(np.float32))
```
